# Optimizing a Trainium2 kernel written in Bass

```python
import math
import jax, jax.numpy as jnp
from jax import lax
import numpy as np

D_MODEL = 1024
BATCH = 16
SEQ = 2048
DEPTH = 1

CHUNK = 64
Q_BLOCK = 128
SB_HEADS = 16
SB_HEAD_DIM = 64
SB_WIDTH = SB_HEADS * SB_HEAD_DIM
CV_WIDTH = D_MODEL
CV_KERNEL = 31
LN_EPS = 1e-5
DEEPNORM_ALPHA = (2.0 * DEPTH) ** 0.25
DEEPNORM_BETA = (8.0 * DEPTH) ** -0.25
SPLIT_SIZES = (SB_WIDTH, SB_WIDTH, SB_WIDTH, SB_WIDTH, CV_WIDTH, CV_WIDTH, CV_WIDTH, D_MODEL, D_MODEL)
IN_WIDTH = sum(SPLIT_SIZES)
SPLIT_POINTS = tuple(int(i) for i in np.cumsum(SPLIT_SIZES)[:-1])

kernel_name = "stickbreak_conformer_gated_hybrid"


def _layer_norm(x, g, b):
    xf = x.astype(jnp.float32)
    mu = jnp.mean(xf, axis=-1, keepdims=True)
    var = jnp.mean(jnp.square(xf - mu), axis=-1, keepdims=True)
    y = (xf - mu) * lax.rsqrt(var + LN_EPS) * g.astype(jnp.float32) + b.astype(jnp.float32)
    return y.astype(x.dtype)


def _stick_breaking_attention(q, k, v):
    seq = q.shape[1]
    scale = 1.0 / math.sqrt(q.shape[-1])
    qf = q.astype(jnp.float32) * scale
    kf = k.astype(jnp.float32)
    vf = v.astype(jnp.float32)
    outs = []
    for start in range(0, seq, Q_BLOCK):
        end = start + Q_BLOCK
        logits = jnp.einsum('bqhd,bkhd->bhqk', qf[:, start:end], kf[:, :end])
        t_idx = start + jnp.arange(Q_BLOCK)[:, None]
        s_idx = jnp.arange(end)[None, :]
        mask = s_idx < t_idx
        log_not_beta = jnp.where(mask, jax.nn.log_sigmoid(-logits), 0.0)
        later = lax.cumsum(log_not_beta, axis=3, reverse=True) - log_not_beta
        weights = jnp.where(mask, jnp.exp(jax.nn.log_sigmoid(logits) + later), 0.0)
        outs.append(jnp.einsum('bhqk,bkhd->bqhd', weights, vf[:, :end]))
    return jnp.concatenate(outs, axis=1).astype(q.dtype)


def _causal_depthwise_conv(u, w, b):
    kw = w.shape[0]
    y = lax.conv_general_dilated(
        u, w[:, None, :].astype(u.dtype), window_strides=(1,), padding=[(kw - 1, 0)],
        dimension_numbers=('NWC', 'WIO', 'NWC'), feature_group_count=u.shape[-1])
    return y + b.astype(u.dtype)


def _hybrid_layer(x, w_in, w_sb_proj, conv_w, conv_b, conv_ln_g, conv_ln_b,
                  w_cv_proj, w_out, ln_g, ln_b):
    bsz, seq, _ = x.shape
    h = x @ w_in
    q, k, v, z_sb, c_val, c_gate, z_cv, g_sb, g_cv = jnp.split(h, SPLIT_POINTS, axis=-1)

    to_heads = lambda t: t.reshape(bsz, seq, SB_HEADS, SB_HEAD_DIM)
    o_sb = _stick_breaking_attention(to_heads(q), to_heads(k), to_heads(v)).reshape(bsz, seq, SB_WIDTH)
    y_sb = (o_sb * jax.nn.silu(z_sb)) @ w_sb_proj

    u = c_val * jax.nn.sigmoid(c_gate)
    u = _causal_depthwise_conv(u, conv_w, conv_b)
    u = jax.nn.silu(_layer_norm(u, conv_ln_g, conv_ln_b))
    y_cv = (u * jax.nn.silu(z_cv)) @ w_cv_proj

    merged = jax.nn.sigmoid(g_sb) * y_sb + jax.nn.sigmoid(g_cv) * y_cv
    return _layer_norm(DEEPNORM_ALPHA * x + merged @ w_out, ln_g, ln_b)


def setup_inputs(seed: int = 0) -> dict:
    key = jax.random.key(seed)
    ks = jax.random.split(key, 14)
    n = lambda kk, shape: jax.random.normal(kk, shape, dtype=jnp.float32)
    x = n(ks[0], (BATCH, SEQ, D_MODEL))
    ln_in_g = 1.0 + 0.02 * n(ks[1], (D_MODEL,))
    ln_in_b = 0.02 * n(ks[2], (D_MODEL,))
    w_in = n(ks[3], (DEPTH, D_MODEL, IN_WIDTH)) * D_MODEL ** -0.5
    w_sb_proj = n(ks[4], (DEPTH, SB_WIDTH, D_MODEL)) * (SB_WIDTH ** -0.5) * DEEPNORM_BETA
    conv_w = n(ks[5], (DEPTH, CV_KERNEL, CV_WIDTH)) * CV_KERNEL ** -0.5
    conv_b = 0.02 * n(ks[6], (DEPTH, CV_WIDTH))
    conv_ln_g = 1.0 + 0.02 * n(ks[7], (DEPTH, CV_WIDTH))
    conv_ln_b = 0.02 * n(ks[8], (DEPTH, CV_WIDTH))
    w_cv_proj = n(ks[9], (DEPTH, CV_WIDTH, D_MODEL)) * (CV_WIDTH ** -0.5) * DEEPNORM_BETA
    w_out = n(ks[10], (DEPTH, D_MODEL, D_MODEL)) * (D_MODEL ** -0.5) * DEEPNORM_BETA
    ln_post_g = 1.0 + 0.02 * n(ks[11], (DEPTH, D_MODEL))
    ln_post_b = 0.02 * n(ks[12], (DEPTH, D_MODEL))
    return {"x": x, "ln_in_g": ln_in_g, "ln_in_b": ln_in_b, "w_in": w_in,
            "w_sb_proj": w_sb_proj, "conv_w": conv_w, "conv_b": conv_b,
            "conv_ln_g": conv_ln_g, "conv_ln_b": conv_ln_b, "w_cv_proj": w_cv_proj,
            "w_out": w_out, "ln_post_g": ln_post_g, "ln_post_b": ln_post_b}


def reference(x, ln_in_g, ln_in_b, w_in, w_sb_proj, conv_w, conv_b, conv_ln_g, conv_ln_b,
              w_cv_proj, w_out, ln_post_g, ln_post_b):
    h = _layer_norm(x, ln_in_g, ln_in_b)
    for l in range(DEPTH):
        h = _hybrid_layer(h, w_in[l], w_sb_proj[l], conv_w[l], conv_b[l], conv_ln_g[l],
                          conv_ln_b[l], w_cv_proj[l], w_out[l], ln_post_g[l], ln_post_b[l])
    return h
```

```python
import numpy as np
from contextlib import ExitStack
import concourse.bass as bass
import concourse.mybir as mybir
from concourse.bass_utils import run_bass_kernel_spmd

F32 = mybir.dt.float32
BF16 = mybir.dt.bfloat16
AF = mybir.ActivationFunctionType
ALU = mybir.AluOpType

S = 2048
D = 1024
NB = 16
NCH = 8
NSEQ = 2
EPS = 1e-5
ALPHA = 2.0 ** 0.25
ENG = ("pe", "act", "dve", "pool", "sp")
NDMA = 8
NWS = 5


def rev_free(ap):
    a = [list(d) for d in ap.ap]
    step, n = a[-1]
    off = ap.offset + step * (n - 1)
    a[-1] = [-step, n]
    return bass.AP(tensor=ap.tensor, offset=off, ap=a)


class Prog:
    def __init__(self, nc, es):
        self.nc = nc
        self.q = {e: [] for e in ENG}
        self.sems = {}
        self.cnt = {}
        for e in ENG:
            self.sems[e] = es.enter_context(nc.semaphore("s_" + e))
            self.cnt[e] = 0
        for qn in ("sp", "pool", "act"):
            for i in range(NDMA):
                k = ("dma", qn, i)
                self.sems[k] = es.enter_context(nc.semaphore("d_%s_%d" % (qn, i)))
                self.cnt[k] = 0
        self.dma_i = {"sp": 0, "pool": 0, "act": 0}
        self.lastw = {}
        self.readers = {}

    def _deps(self, reads, writes):
        waits = {}

        def need(ev):
            k, c = ev
            if waits.get(k, 0) < c:
                waits[k] = c

        for k in reads:
            if k in self.lastw:
                need(self.lastw[k])
        for k in writes:
            if k in self.lastw:
                need(self.lastw[k])
            for r in self.readers.get(k, ()):
                need(r)
        return waits

    def _commit(self, ev, reads, writes):
        for k in reads:
            self.readers.setdefault(k, []).append(ev)
        for k in writes:
            self.lastw[k] = ev
            self.readers[k] = []

    def op(self, eng, fn, reads=(), writes=()):
        waits = self._deps(reads, writes)
        self.cnt[eng] += 1
        ev = (eng, self.cnt[eng])
        self._commit(ev, reads, writes)
        self.q[eng].append((waits, fn, self.sems[eng], 1))
        return ev

    def dma(self, qn, fn, reads=(), writes=()):
        waits = self._deps(reads, writes)
        i = self.dma_i[qn] % NDMA
        self.dma_i[qn] += 1
        k = ("dma", qn, i)
        if self.cnt[k] > 0:
            if waits.get(k, 0) < self.cnt[k]:
                waits[k] = self.cnt[k]
        self.cnt[k] += 16
        ev = (k, self.cnt[k])
        self._commit(ev, reads, writes)
        self.q[qn].append((waits, fn, self.sems[k], 16))
        return ev

    def barrier(self):
        allk = {k: c for k, c in self.cnt.items() if c > 0}
        for e in ENG:
            self.q[e].append((dict(allk), None, None, 0))
        self.lastw = {}
        self.readers = {}

    def replay(self, block):
        prog = self

        def run(eng_name):
            def body(e):
                waited = {}
                for waits, fn, sem, inc in prog.q[eng_name]:
                    for k, c in waits.items():
                        if waited.get(k, 0) < c:
                            e.wait_ge(prog.sems[k], c)
                            waited[k] = c
                    if fn is not None:
                        ins = fn(e)
                        ins.then_inc(sem, inc)
            return body

        block.tensor(run("pe"))
        block.scalar(run("act"))
        block.vector(run("dve"))
        block.gpsimd(run("pool"))
        block.sync(run("sp"))


def build_program():
    nc = bass.Bass("TRN2", target_bir_lowering=False)
    x_d = nc.dram_tensor("x", [NSEQ, S, D], F32, kind="ExternalInput").ap()
    ln_in_g = nc.dram_tensor("ln_in_g", [D], F32, kind="ExternalInput").ap()
    ln_in_b = nc.dram_tensor("ln_in_b", [D], F32, kind="ExternalInput").ap()
    w_in = nc.dram_tensor("w_in", [D, 9 * D], F32, kind="ExternalInput").ap()
    w_sb = nc.dram_tensor("w_sb_proj", [D, D], F32, kind="ExternalInput").ap()
    conv_w = nc.dram_tensor("conv_w", [31, D], F32, kind="ExternalInput").ap()
    conv_b = nc.dram_tensor("conv_b", [D], F32, kind="ExternalInput").ap()
    cln_g = nc.dram_tensor("conv_ln_g", [D], F32, kind="ExternalInput").ap()
    cln_b = nc.dram_tensor("conv_ln_b", [D], F32, kind="ExternalInput").ap()
    w_cv = nc.dram_tensor("w_cv_proj", [D, D], F32, kind="ExternalInput").ap()
    w_out = nc.dram_tensor("w_out", [D, D], F32, kind="ExternalInput").ap()
    ln_post_g = nc.dram_tensor("ln_post_g", [D], F32, kind="ExternalInput").ap()
    ln_post_b = nc.dram_tensor("ln_post_b", [D], F32, kind="ExternalInput").ap()
    y_d = nc.dram_tensor("y", [NSEQ, S, D], F32, kind="ExternalOutput").ap()

    with ExitStack() as es:
        E = es.enter_context

        def sb(name, shape, dt):
            return E(nc.sbuf_tensor(name, shape, dt))

        identb = sb("identb", [128, 128], BF16)
        Jb = sb("Jb", [128, 128], BF16)
        ones32 = sb("ones32", [128, 128], F32)
        epsb = sb("epsb", [128, 1], F32)
        cw = sb("cw", [128, NCH * 31], F32)
        cvec = sb("cvec", [128, 3 * NCH], F32)
        hT = sb("hT", [128, NCH * S], BF16)
        R1 = sb("R1", [128, NCH * S], BF16)
        R2 = sb("R2", [128, NCH * S], BF16)
        ws = [sb("ws%d" % i, [128, NCH * 128], BF16) for i in range(NWS)]
        sg = [sb("sg%d" % i, [128, 512], F32) for i in range(2)]
        st6 = [sb("st6_%d" % i, [128, 12], F32) for i in range(10)]
        mv = [sb("mv%d" % i, [128, 2], F32) for i in range(10)]
        rstd = [sb("rstd%d" % i, [128, 1], F32) for i in range(10)]
        nmr = [sb("nmr%d" % i, [128, 1], F32) for i in range(10)]
        rstdP = sb("rstdP", [128, NB], F32)
        nmrP = sb("nmrP", [128, NB], F32)
        UB = 96256
        U = sb("U", [128, UB // 2], BF16)

        def uview(off, nelem, dt):
            if dt == BF16:
                return U[:, off // 2: off // 2 + nelem]
            return U[:, off // 2: off // 2 + 2 * nelem].bitcast(F32)

        lg = [E(nc.psum_tensor("lg%d" % i, [128, 1024], F32)) for i in range(2)]
        pt = [E(nc.psum_tensor("pt%d" % i, [128, 1024], BF16)) for i in range(2)]
        po = [E(nc.psum_tensor("po%d" % i, [128, 512], F32)) for i in range(2)]
        PP = [(lg[0][:, 0:512], "lg0a"), (lg[0][:, 512:1024], "lg0b"),
              (lg[1][:, 0:512], "lg1a"), (lg[1][:, 512:1024], "lg1b"),
              (po[0][:, :], "po0"), (po[1][:, :], "po1")]

        P = Prog(nc, es)
        hT3 = hT[:, :].rearrange("p (c t) -> p c t", c=NCH)
        vrev3 = R1[:, :].rearrange("p (b f) -> p b f", b=NB)
        m1T3 = R1[:, :].rearrange("p (c t) -> p c t", c=NCH)
        gT3 = R2[:, :].rearrange("p (c t) -> p c t", c=NCH)
        ws3 = [w[:, :].rearrange("p (k n) -> p k n", k=NCH) for w in ws]
        cw3 = cw[:, :].rearrange("p (c j) -> p c j", c=NCH)

        io1 = uview(94800, 128, F32)
        io2 = uview(94800 + 512, 128, F32)
        P.op("pool", lambda e: e.iota(io1, pattern=[[1, 128]], base=0, channel_multiplier=-1,
                                      allow_small_or_imprecise_dtypes=True), writes=["io1"])
        P.op("pool", lambda e: e.iota(io2, pattern=[[1, 128]], base=-127, channel_multiplier=1,
                                      allow_small_or_imprecise_dtypes=True), writes=["io2"])
        P.op("pool", lambda e: e.memset(ones32[:, :], 1.0), writes=["ones32"])
        P.op("pool", lambda e: e.memset(epsb[:, :], EPS), writes=["epsb"])
        P.op("dve", lambda e: e.tensor_single_scalar(out=identb[:, :], in_=io1, scalar=0.0, op=ALU.is_equal),
             reads=["io1"], writes=["identb"])
        P.op("dve", lambda e: e.tensor_single_scalar(out=Jb[:, :], in_=io2, scalar=0.0, op=ALU.is_equal),
             reads=["io2"], writes=["Jb"])
        cwrow = uview(55296, 1024, F32)
        cvrow = uview(59392, 1024, F32)
        ident32 = uview(63488, 128, F32)
        P.op("dve", lambda e: e.tensor_single_scalar(out=ident32, in_=io1, scalar=0.0, op=ALU.is_equal),
             reads=["io1"], writes=["ident32"])
        P.dma("sp", lambda e: e.dma_start(out=cwrow[0:31, :], in_=conv_w), writes=["cwrow"])
        for i, v in enumerate((conv_b, cln_g, cln_b)):
            P.dma("sp", lambda e, i=i, v=v: e.dma_start(out=cvrow[i:i + 1, :], in_=v.rearrange("(o n) -> o n", o=1)),
                  writes=[("cvrow", i)])

        def fcw(e):
            ins = None
            for cc in range(NCH):
                ins = e.matmul(out=po[0][:, cc * 32: cc * 32 + 31], lhsT=cwrow[0:31, cc * 128:(cc + 1) * 128],
                               rhs=ident32[0:31, 0:31], start=True, stop=True)
            return ins
        P.op("pe", fcw, reads=["cwrow", "ident32"], writes=[("ps", "po0")])
        P.op("dve", lambda e: e.tensor_copy(out=cw3, in_=po[0][:, 0:256].rearrange("p (c j) -> p c j", c=NCH)[:, :, 0:31]),
             reads=[("ps", "po0")], writes=["cw"])

        def fcv0(e):
            ins = None
            for cc in range(NCH):
                ins = e.matmul(out=po[1][:, cc * 4: cc * 4 + 3], lhsT=cvrow[0:3, cc * 128:(cc + 1) * 128],
                               rhs=ident32[0:3, 0:3], start=True, stop=True)
            return ins
        P.op("pe", fcv0, reads=[("cvrow", 0), ("cvrow", 1), ("cvrow", 2), "ident32"], writes=[("ps", "po1")])
        P.op("dve", lambda e: e.tensor_copy(
            out=cvec[:, :].rearrange("p (i c) -> p i c", i=3),
            in_=po[1][:, 0:32].rearrange("p (c i) -> p c i", c=NCH)[:, :, 0:3].rearrange("p c i -> p i c")),
            reads=[("ps", "po1")], writes=["cvec"])
        P.barrier()

        class WStream:
            def __init__(self, specs):
                self.specs = specs
                self.loaded = 0
                self.base = WStream.gbase
                WStream.gbase += len(specs)

            def ensure(self, upto):
                while self.loaded < min(upto + 1, len(self.specs)):
                    i = self.loaded
                    slot = (self.base + i) % NWS
                    src = self.specs[i].rearrange("(k p) n -> p k n", p=128)
                    P.dma("pool", lambda e, slot=slot, src=src: e.dma_start(out=ws3[slot], in_=src),
                          writes=[("ws", slot)])
                    self.loaded += 1

            def get(self, i):
                self.ensure(i + NWS - 2)
                slot = (self.base + i) % NWS
                return ws3[slot], ("ws", slot)

        WStream.gbase = 0
        acc_i = [0]
        ev_i = [0]

        def next_acc(n=len(PP)):
            a = PP[acc_i[0] % n]
            acc_i[0] += 1
            return a

        def ln_stats(xt, xk, b):
            P.op("dve", lambda e: e.bn_stats(out=st6[b][:, 0:6], in_=xt[:, 0:512]), reads=[xk], writes=[("st6a", b)])
            P.op("dve", lambda e: e.bn_stats(out=st6[b][:, 6:12], in_=xt[:, 512:1024]), reads=[xk], writes=[("st6b", b)])
            P.op("dve", lambda e: e.bn_aggr(out=mv[b][:, :], in_=st6[b][:, :]),
                 reads=[("st6a", b), ("st6b", b)], writes=[("mv", b)])

        def ln_sqrt(b, rs=None, rk=None):
            rs = rstd[b][:, :] if rs is None else rs
            rk = ("rstd", b) if rk is None else rk
            P.op("act", lambda e: e.activation(out=rs, in_=mv[b][:, 1:2], func=AF.Sqrt,
                                               bias=epsb[:, :], scale=1.0),
                 reads=[("mv", b), "epsb"], writes=[rk])

        def ln_recip(b, rs=None, rk=None, nm=None, nk=None):
            rs = rstd[b][:, :] if rs is None else rs
            rk = ("rstd", b) if rk is None else rk
            nm = nmr[b][:, :] if nm is None else nm
            nk = ("nmr", b) if nk is None else nk
            P.op("dve", lambda e: e.reciprocal(out=rs, in_=rs), reads=[rk], writes=[rk])
            P.op("dve", lambda e: e.scalar_tensor_tensor(out=nm, in0=mv[b][:, 0:1], scalar=-1.0,
                                                         in1=rs, op0=ALU.mult, op1=ALU.mult),
                 reads=[("mv", b), rk], writes=[nk])

        def ln_norm(xt, xk, t1, t1k, b, rs=None, rk=None, nm=None, nk=None):
            rs = rstd[b][:, :] if rs is None else rs
            rk = ("rstd", b) if rk is None else rk
            nm = nmr[b][:, :] if nm is None else nm
            nk = ("nmr", b) if nk is None else nk
            P.op("act", lambda e: e.activation(out=t1, in_=xt, func=AF.Identity, bias=nm, scale=rs),
                 reads=[xk, rk, nk], writes=[t1k])

        def ln_mulg(t1, t1k, gbc):
            P.op("dve", lambda e: e.tensor_tensor(out=t1, in0=t1, in1=gbc, op=ALU.mult),
                 reads=[t1k, "gb"], writes=[t1k])

        def ln_addb(t1, t1k, bbc, out_ap, out_key):
            P.op("pool", lambda e: e.tensor_tensor(out=out_ap, in0=t1, in1=bbc, op=ALU.add),
                 reads=[t1k, "gb"], writes=[out_key])

        def run_pipeline(stages, nblocks):
            for k in range(nblocks + len(stages) - 1):
                for si, fn_ in enumerate(stages):
                    if 0 <= k - si < nblocks:
                        fn_(k - si)

        def proj_fm(wt, wk, rhs3, rkeyf, t4):
            acc, ak = next_acc()

            def f(e, acc=acc, wt=wt, t4=t4):
                ins = None
                for kc in range(NCH):
                    ins = e.matmul(out=acc, lhsT=wt[:, kc, :], rhs=rhs3[:, kc, t4 * 512:(t4 + 1) * 512],
                                   start=(kc == 0), stop=(kc == NCH - 1))
                return ins
            P.op("pe", f, reads=[wk] + rkeyf(t4), writes=[("ps", ak)])
            return acc, ("ps", ak)

        def hkeys(t4):
            return [("hT", t4 * 4 + j) for j in range(4)]

        for s in range(NSEQ):
            NXT, NT1, NHB, NST = 5, 4, 3, 5
            xts = [uview(i * 4096, 1024, F32) for i in range(NXT)]
            t1s = [uview(20480 + i * 4096, 1024, F32) for i in range(NT1)]
            hbs = [uview(36864 + i * 2048, 1024, BF16) for i in range(NHB)]
            g_in = uview(43008, 1024, F32)
            b_in = uview(47104, 1024, F32)
            P.dma("sp", lambda e, g_in=g_in: e.dma_start(out=g_in, in_=ln_in_g.partition_broadcast(128)), writes=["gb"])
            P.dma("sp", lambda e, b_in=b_in: e.dma_start(out=b_in, in_=ln_in_b.partition_broadcast(128)), writes=["gb"])

            def q0(tb):
                xt = xts[tb % NXT]
                P.dma("sp", lambda e, xt=xt, tb=tb, s=s: e.dma_start(out=xt, in_=x_d[s, tb * 128:(tb + 1) * 128, :]),
                      writes=[("xt", tb % NXT)])

            def q1(tb):
                ln_stats(xts[tb % NXT], ("xt", tb % NXT), tb % NST)

            def q2(tb):
                ln_sqrt(tb % NST, rstdP[:, tb:tb + 1], ("rstdP", tb))

            def q3(tb):
                ln_recip(tb % NST, rstdP[:, tb:tb + 1], ("rstdP", tb), nmrP[:, tb:tb + 1], ("nmrP", tb))

            def q4(tb):
                ln_norm(xts[tb % NXT], ("xt", tb % NXT), t1s[tb % NT1], ("t1", tb % NT1), tb % NST,
                        rstdP[:, tb:tb + 1], ("rstdP", tb), nmrP[:, tb:tb + 1], ("nmrP", tb))

            def q5(tb):
                ln_mulg(t1s[tb % NT1], ("t1", tb % NT1), g_in)

            def q6(tb):
                ln_addb(t1s[tb % NT1], ("t1", tb % NT1), b_in, hbs[tb % NHB], ("hb", tb % NHB))

            def q7(tb):
                b = tb % 2
                hb = hbs[tb % NHB]

                def ftr(e, hb=hb, b=b):
                    ins = None
                    for fc in range(NCH):
                        ins = e.matmul(out=lg[b][:, fc * 128:(fc + 1) * 128], lhsT=hb[:, fc * 128:(fc + 1) * 128],
                                       rhs=Jb[:, :], start=True, stop=True)
                    return ins
                P.op("pe", ftr, reads=[("hb", tb % NHB), "Jb"], writes=[("ps", "lg%da" % b), ("ps", "lg%db" % b)])

            def q8(tb):
                b = tb % 2
                tr = NB - 1 - tb
                P.op("act", lambda e, b=b, tr=tr: e.activation(out=hT3[:, :, tr * 128:(tr + 1) * 128],
                                                              in_=lg[b][:, :].rearrange("p (c t) -> p c t", c=NCH), func=AF.Copy),
                     reads=[("ps", "lg%da" % b), ("ps", "lg%db" % b)], writes=[("hT", tr)])

            Wv = WStream([w_in[:, 2 * D + c * 128: 2 * D + (c + 1) * 128] for _g in range(4) for c in range(NCH)])
            vcnt = [0]

            vitems = [(g4, c) for g4 in (3, 2, 1, 0) for c in range(NCH)]
            vnext = [0]

            def emit_v(i):
                g4, c = vitems[i]
                wt, wk = Wv.get(i)
                acc, ak = PP[4 + i % 2]

                def fv(e, acc=acc, wt=wt, g4=g4):
                    ins = None
                    for j in range(4):
                        tb_ = g4 * 4 + j
                        for kc in range(NCH):
                            ins = e.matmul(out=acc[:, j * 128:(j + 1) * 128], lhsT=hT3[:, kc, tb_ * 128:(tb_ + 1) * 128],
                                           rhs=wt[:, kc, :], start=(kc == 0), stop=(kc == NCH - 1))
                    return ins
                P.op("pe", fv, reads=[wk] + hkeys(g4), writes=[("ps", ak)])
                dst = vrev3[:, g4 * 4:(g4 + 1) * 4, c * 128:(c + 1) * 128]
                src = acc.rearrange("p (j f) -> p j f", j=4)
                P.op("act", lambda e, dst=dst, src=src: e.activation(out=dst, in_=src, func=AF.Copy),
                     reads=[("ps", ak)], writes=[("v", c, g4)])

            def q9(tb):
                for _ in range(2):
                    i = vnext[0]
                    if i >= len(vitems):
                        return
                    g4, _c = vitems[i]
                    if NB - 1 - 4 * g4 > tb:
                        return
                    emit_v(i)
                    vnext[0] += 1

            run_pipeline([q0, q1, q2, q3, q4, q5, q6, q7, q8, q9], NB)
            while vnext[0] < len(vitems):
                emit_v(vnext[0])
                vnext[0] += 1
            specs = []
            for c in range(NCH):
                for goff in (0, D, 3 * D):
                    specs.append(w_in[:, goff + c * 128: goff + (c + 1) * 128])
            W = WStream(specs)
            W.ensure(NWS - 3)
            P.barrier()

            qT = [uview(i * 4096, S, BF16) for i in range(2)]
            kT = [uview(8192 + i * 4096, S, BF16) for i in range(2)]
            szT = [uview(16384 + i * 4096, S, BF16) for i in range(2)]
            NPIX = 4
            PiX = [uview(24576 + i * 8208, S + 1, F32) for i in range(NPIX)]
            NWB = 4
            wb = [uview(57408 + i * 4096, S, BF16) for i in range(NWB)]
            NWT = 8
            wT = [uview(73792 + i * 2048, 1024, BF16) for i in range(NWT)]
            negmask = uview(94288, 128, BF16)
            P.op("pool", lambda e: e.iota(io1, pattern=[[1, 128]], base=0, channel_multiplier=-1,
                                          allow_small_or_imprecise_dtypes=True), writes=["io1"])
            P.op("dve", lambda e, negmask=negmask: e.tensor_scalar(out=negmask, in0=io1, scalar1=0.0, scalar2=-30000.0,
                                                                  op0=ALU.is_le, op1=ALU.mult),
                 reads=["io1"], writes=["maskd"])
            for i in range(NPIX):
                P.op("pool", lambda e, i=i, PiX=PiX: e.memset(PiX[i][:, 0:1], 1.0), writes=[("pix0", i)])

            NLB = 2
            cnt = {"piece": 0, "grp": 0}
            czero = po[1][:, 0:1]
            P.op("dve", lambda e: e.memset(czero, 0.0), writes=["czero"])

            def emit_proj(c, gi, t4):
                cb = c % 2
                wt, wk = W.get(c * 3 + gi)
                dstT = (qT[cb], kT[cb], szT[cb])[gi]
                sl_ = cnt["piece"] % NLB
                cnt["piece"] += 1
                pacc = lg[sl_][:, 0:512]
                pacck = ("ps", "lg%da" % sl_)

                def f(e, wt=wt, t4=t4, pacc=pacc):
                    ins = None
                    for kc in range(NCH):
                        ins = e.matmul(out=pacc, lhsT=wt[:, kc, :], rhs=hT3[:, kc, t4 * 512:(t4 + 1) * 512],
                                       start=(kc == 0), stop=(kc == NCH - 1))
                    return ins
                P.op("pe", f, reads=[wk] + hkeys(t4), writes=[pacck, ("ps", "lg%db" % sl_)])
                dst = dstT[:, t4 * 512:(t4 + 1) * 512]
                dk = (("q", "k", "sz")[gi], cb, t4)
                if gi == 0:
                    P.op("act", lambda e, dst=dst, pacc=pacc: e.activation(out=dst, in_=pacc, func=AF.Copy),
                         reads=[pacck], writes=[dk])
                elif gi == 1:
                    P.op("act", lambda e, dst=dst, pacc=pacc: e.activation(out=dst, in_=pacc, func=AF.Copy),
                         reads=[pacck], writes=[dk])
                else:
                    sgi = ev_i[0] % 2
                    ev_i[0] += 1
                    P.op("act", lambda e, sgi=sgi, pacc=pacc: e.activation(out=sg[sgi][:, :], in_=pacc, func=AF.Sigmoid),
                         reads=[pacck], writes=[("sg", sgi)])
                    P.op("act", lambda e, sgi=sgi, pacc=pacc: e.activation(out=sg[1 - sgi][:, :], in_=pacc, func=AF.Copy),
                         reads=[pacck], writes=[("sg", 1 - sgi)])
                    P.op("pool", lambda e, dst=dst, sgi=sgi: e.tensor_tensor(out=dst, in0=sg[1 - sgi][:, :], in1=sg[sgi][:, :], op=ALU.mult),
                         reads=[("sg", 0), ("sg", 1)], writes=[dk])

            proj_list = [(gi, t4) for gi in range(3) for t4 in range(4)]
            for gi, t4 in proj_list:
                emit_proj(0, gi, t4)

            items = [(c, qb, hh) for c in range(NCH) for qb in range(NB) for hh in range(2)]
            NI = len(items)
            info = {}

            def S1(n):
                c, qb, hh = items[n]
                cb = c % 2
                col0 = qb * 128
                L = (NB - qb) * 128
                ib = n % NPIX
                hs = slice(hh * 64, (hh + 1) * 64)
                npiece = (L + 1023) // 1024
                for p_ in range(npiece):
                    sl_ = cnt["piece"] % NLB
                    cnt["piece"] += 1
                    c0 = col0 + p_ * 1024
                    m = min(1024, L - p_ * 1024)
                    lk = [("ps", "lg%da" % sl_), ("ps", "lg%db" % sl_)]

                    def fqk(e, sl_=sl_, hs=hs, col0=col0, cb=cb, c0=c0, m=m, p_=p_):
                        ins = None
                        for sub in range((m + 511) // 512):
                            mm = min(512, m - sub * 512)
                            lo_ = 0
                            if p_ == 0 and sub == 0:
                                e.matmul(out=lg[sl_][:, 0:128], lhsT=qT[cb][hs, col0:col0 + 128],
                                         rhs=kT[cb][hs, c0:c0 + 128], start=True, stop=False)
                                ins = e.matmul(out=lg[sl_][:, 0:128], lhsT=identb[:, :], rhs=negmask[:, :], start=False, stop=True)
                                lo_ = 128
                            if mm > lo_:
                                ins = e.matmul(out=lg[sl_][:, sub * 512 + lo_: sub * 512 + mm], lhsT=qT[cb][hs, col0:col0 + 128],
                                               rhs=kT[cb][hs, c0 + sub * 512 + lo_: c0 + sub * 512 + mm], start=True, stop=True)
                        return ins
                    kk = sorted(set(("k", cb, (c0 + o) // 512) for o in range(0, m, 128)))
                    P.op("pe", fqk, reads=[("q", cb, qb // 4), "maskd", "identb"] + kk, writes=lk)
                    P.op("act", lambda e, sl_=sl_, p_=p_, ib=ib, m=m: e.activation(
                        out=PiX[ib][:, 1 + p_ * 1024: 1 + p_ * 1024 + m], in_=lg[sl_][:, 0:m], func=AF.Sigmoid, scale=-0.125),
                        reads=lk, writes=[("pix", ib, p_)])

            def S23(n):
                c, qb, hh = items[n]
                L = (NB - qb) * 128
                ib = n % NPIX
                iw = n % NWB
                pixk = [("pix", ib, p_) for p_ in range(2)]
                zb = bass.AP(tensor=czero.tensor, offset=czero.offset, ap=[list(czero.ap[0]), [0, L]])
                P.op("dve", lambda e, ib=ib, L=L, zb=zb: e.tensor_tensor_scan(
                    out=PiX[ib][:, 1:L + 1], data0=PiX[ib][:, 1:L + 1], data1=zb,
                    initial=1.0, op0=ALU.mult, op1=ALU.max),
                    reads=pixk + ["czero"], writes=pixk)
                P.op("pool", lambda e, ib=ib, iw=iw, L=L: e.tensor_tensor(
                    out=wb[iw][:, 0:L], in0=PiX[ib][:, 0:L], in1=PiX[ib][:, 1:L + 1], op=ALU.subtract),
                    reads=pixk + [("pix0", ib)], writes=[("wb", iw)])

            def S4(n):
                c, qb, hh = items[n]
                nblk = NB - qb
                iw = n % NWB
                groups = []
                for g in range((nblk + 7) // 8):
                    nbg = min(8, nblk - g * 8)
                    gp_ = cnt["grp"] % 2
                    gw = cnt["grp"] % NWT
                    gi_ = cnt["grp"]
                    cnt["grp"] += 1
                    groups.append((g, nbg, gw))

                    def ft(e, iw=iw, g=g, nbg=nbg, gp_=gp_):
                        ins = None
                        for j in range(nbg):
                            ins = e.transpose(out=pt[gp_][:, j * 128:(j + 1) * 128],
                                              in_=wb[iw][:, (g * 8 + j) * 128:(g * 8 + j + 1) * 128], identity=identb[:, :])
                        return ins
                    P.op("pe", ft, reads=[("wb", iw), "identb"], writes=[("ps", "pt%d" % gp_)])
                    if True:
                        P.op("act", lambda e, gw=gw, gp_=gp_, nbg=nbg: e.activation(
                            out=wT[gw][:, 0:nbg * 128], in_=pt[gp_][:, 0:nbg * 128], func=AF.Copy),
                            reads=[("ps", "pt%d" % gp_)], writes=[("wT", gw)])
                    else:
                        P.op("dve", lambda e, gw=gw, gp_=gp_, nbg=nbg: e.tensor_copy(
                            out=wT[gw][:, 0:nbg * 128], in_=pt[gp_][:, 0:nbg * 128]),
                            reads=[("ps", "pt%d" % gp_)], writes=[("wT", gw)])
                info[n] = groups

            def S5(n):
                c, qb, hh = items[n]
                assert hh == 1
                cb = c % 2
                col0 = qb * 128
                nblk = NB - qb
                g0 = info.pop(n - 1)
                g1 = info.pop(n)
                for (g, nbg, gw0), (_, _, gw1) in zip(g0, g1):
                    def fpv(e, gw0=gw0, gw1=gw1, g=g, nbg=nbg, qb=qb, c=c, nblk=nblk):
                        ins = None
                        for j in range(nbg):
                            kb = qb + g * 8 + j
                            first = (g == 0 and j == 0)
                            last = (g * 8 + j == nblk - 1)
                            e.matmul(out=po[0][0:64, 0:128], lhsT=vrev3[:, kb, c * 128: c * 128 + 64],
                                     rhs=wT[gw0][:, j * 128:(j + 1) * 128], start=first, stop=last)
                            ins = e.matmul(out=po[0][64:128, 0:128], lhsT=vrev3[:, kb, c * 128 + 64: c * 128 + 128],
                                           rhs=wT[gw1][:, j * 128:(j + 1) * 128], start=first, stop=last,
                                           tile_position=(0, 64))
                        return ins
                    vkeys = sorted(set(("v", c, (qb + g * 8 + j) // 4) for j in range(nbg)))
                    P.op("pe", fpv, reads=[("wT", gw0), ("wT", gw1)] + vkeys, writes=[("ps", "po0")])
                P.op("dve", lambda e, c=c, col0=col0, cb=cb: e.tensor_tensor(
                    out=gT3[:, c, col0:col0 + 128], in0=po[0][:, 0:128], in1=szT[cb][:, col0:col0 + 128], op=ALU.mult),
                    reads=[("ps", "po0"), ("sz", cb, qb // 4)], writes=[("gT", c, qb // 4)])

            LAG_T = 3
            LAG_PV = 4
            for n in range(NI + LAG_PV):
                if n < NI:
                    c, qb, hh = items[n]
                    if c + 1 < NCH and hh == 0 and 2 <= qb < 2 + len(proj_list):
                        gi, t4 = proj_list[qb - 2]
                        emit_proj(c + 1, gi, t4)
                    S1(n)
                if 0 <= n - LAG_T < NI:
                    S4(n - LAG_T)
                if 0 <= n - LAG_PV < NI and (n - LAG_PV) % 2 == 1:
                    S5(n - LAG_PV)
                if n < NI:
                    S23(n)
            specs = []
            for c in range(NCH):
                specs.append(w_in[:, 7 * D + c * 128: 7 * D + (c + 1) * 128])
                specs.append(w_sb[:, c * 128:(c + 1) * 128])
            W = WStream(specs)
            W.ensure(NWS - 3)
            P.barrier()


            def gkeys(t4):
                return [("gT", c_, t4) for c_ in range(NCH)]
            for c in range(NCH):
                wg, wgk = W.get(2 * c)
                wy, wyk = W.get(2 * c + 1)
                for t4 in range(4):
                    accg, agk = proj_fm(wg, wgk, hT3, hkeys, t4)
                    accy, ayk = proj_fm(wy, wyk, gT3, gkeys, t4)
                    sgi = ev_i[0] % 2
                    ev_i[0] += 1
                    P.op("act", lambda e, accg=accg, sgi=sgi: e.activation(out=sg[sgi][:, :], in_=accg, func=AF.Sigmoid),
                         reads=[agk], writes=[("sg", sgi)])
                    P.op("dve", lambda e, accy=accy, sgi=sgi, c=c, t4=t4: e.tensor_tensor(
                        out=m1T3[:, c, t4 * 512:(t4 + 1) * 512], in0=accy, in1=sg[sgi][:, :], op=ALU.mult),
                        reads=[ayk, ("sg", sgi)], writes=[("m1", c, t4)])

            ucT = uview(0, NCH * S, F32)
            ucT3 = ucT.rearrange("p (c t) -> p c t", c=NCH)
            uT = [uview(65536 + i * 4160, S + 32, BF16) for i in range(2)]
            Dg = uview(73856, 31 * 128, BF16)
            Dg3 = Dg.rearrange("p (j n) -> p j n", j=31)
            meanT = uview(81792, 512, F32)
            rstdT = uview(81792 + 2048, 512, F32)
            tA = uview(81792 + 4096, 512, F32)
            tB = uview(81792 + 6144, 512, F32)
            pT3 = gT3
            S1 = R2[:, 8192:12288].bitcast(F32)
            S2 = R2[:, 12288:16384].bitcast(F32)
            sqt = [R2[:, 6144 + i * 1024: 6144 + (i + 1) * 1024].bitcast(F32) for i in range(2)]
            for i in range(2):
                P.op("pool", lambda e, i=i: e.memset(uT[i][:, S:S + 32], 0.0), writes=[("upad", i)])
            specs = []
            for c in range(NCH):
                specs.append(w_in[:, 4 * D + c * 128: 4 * D + (c + 1) * 128])
                specs.append(w_in[:, 5 * D + c * 128: 5 * D + (c + 1) * 128])
            W = WStream(specs)
            def projU(cc):
                ub = cc % 2
                wv_, wvk = W.get(2 * cc)
                wg, wgk = W.get(2 * cc + 1)
                for t4 in range(4):
                    accv, avk = proj_fm(wv_, wvk, hT3, hkeys, t4)
                    accg, agk = proj_fm(wg, wgk, hT3, hkeys, t4)
                    sgi = ev_i[0] % 2
                    ev_i[0] += 1
                    P.op("act", lambda e, accg=accg, sgi=sgi: e.activation(out=sg[sgi][:, :], in_=accg, func=AF.Sigmoid),
                         reads=[agk], writes=[("sg", sgi)])
                    P.op("dve", lambda e, accv=accv, sgi=sgi, ub=ub, t4=t4: e.tensor_tensor(
                        out=uT[ub][:, t4 * 512:(t4 + 1) * 512], in0=accv, in1=sg[sgi][:, :], op=ALU.mult),
                        reads=[avk, ("sg", sgi)], writes=[("uT", ub, t4)])

            def buildDg(cc):
                ida = identb[:, :]
                idb3 = bass.AP(tensor=ida.tensor, offset=ida.offset, ap=[list(ida.ap[0]), [0, 31], list(ida.ap[1])])
                cwa = cw3[:, cc, :]
                cwb3 = bass.AP(tensor=cwa.tensor, offset=cwa.offset, ap=[list(cwa.ap[0]), list(cwa.ap[1]), [0, 128]])
                P.op("dve", lambda e, idb3=idb3, cwb3=cwb3: e.tensor_tensor(out=Dg3, in0=idb3, in1=cwb3, op=ALU.mult),
                     reads=["identb", "cw"], writes=["Dg"])

            def convU(cc):
                ub = cc % 2
                for t4 in range(4):
                    acc, ak = next_acc()

                    def fcv(e, acc=acc, ub=ub, t4=t4):
                        ins = None
                        for j in range(31):
                            o = t4 * 512 + 30 - j
                            ins = e.matmul(out=acc, lhsT=Dg3[:, j, :], rhs=uT[ub][:, o:o + 512], start=(j == 0), stop=(j == 30))
                        return ins
                    uk = [("uT", ub, t4), ("upad", ub)] + ([("uT", ub, t4 + 1)] if t4 < 3 else [])
                    P.op("pe", fcv, reads=uk + ["Dg"], writes=[("ps", ak)])
                    P.op("act", lambda e, acc=acc, cc=cc, t4=t4: e.activation(
                        out=ucT3[:, cc, t4 * 512:(t4 + 1) * 512], in_=acc, func=AF.Identity,
                        bias=cvec[:, cc:cc + 1], scale=1.0),
                        reads=[("ps", ak), "cvec"], writes=[("uc", cc, t4)])
                    tsl = slice(t4 * 512, (t4 + 1) * 512)
                    ucv = ucT3[:, cc, tsl]
                    if cc == 0:
                        P.op("dve", lambda e, ucv=ucv, tsl=tsl: e.tensor_copy(out=S1[:, tsl], in_=ucv),
                             reads=[("uc", cc, t4)], writes=[("S1", t4)])
                        P.op("dve", lambda e, ucv=ucv, tsl=tsl: e.tensor_tensor(out=S2[:, tsl], in0=ucv, in1=ucv, op=ALU.mult),
                             reads=[("uc", cc, t4)], writes=[("S2", t4)])
                    else:
                        qi = (cc * 4 + t4) % 2
                        P.op("dve", lambda e, ucv=ucv, tsl=tsl: e.tensor_tensor(out=S1[:, tsl], in0=S1[:, tsl], in1=ucv, op=ALU.add),
                             reads=[("uc", cc, t4), ("S1", t4)], writes=[("S1", t4)])
                        P.op("dve", lambda e, ucv=ucv, qi=qi: e.tensor_tensor(out=sqt[qi], in0=ucv, in1=ucv, op=ALU.mult),
                             reads=[("uc", cc, t4)], writes=[("sqt", qi)])
                        P.op("dve", lambda e, tsl=tsl, qi=qi: e.tensor_tensor(out=S2[:, tsl], in0=S2[:, tsl], in1=sqt[qi], op=ALU.add),
                             reads=[("sqt", qi), ("S2", t4)], writes=[("S2", t4)])

            buildDg(0)
            projU(0)
            for cc in range(NCH):
                if cc + 1 < NCH:
                    projU(cc + 1)
                convU(cc)
                if cc + 1 < NCH:
                    buildDg(cc + 1)
            Wz = WStream([w_in[:, 6 * D + cc * 128: 6 * D + (cc + 1) * 128] for cc in range(NCH)])
            Wz.ensure(NWS - 3)
            P.barrier()
            meanA = uview(65536, S, F32)
            rstdA = uview(65536 + 8192, S, F32)
            tmp6 = [uview(81920 + i * 2048, 512, F32) for i in range(6)]
            sqeng = ("act", "pool", "dve")
            for t4 in range(4):
                tsl = slice(t4 * 512, (t4 + 1) * 512)
                s1, s1k = PP[(2 * t4) % 6]
                s2, s2k = PP[(2 * t4 + 1) % 6]
                P.op("pe", lambda e, s1=s1, tsl=tsl: e.matmul(out=s1, lhsT=ones32[:, :], rhs=S1[:, tsl], start=True, stop=True),
                     reads=["S1all", "ones32"], writes=[("ps", s1k)])
                P.op("pe", lambda e, s2=s2, tsl=tsl: e.matmul(out=s2, lhsT=ones32[:, :], rhs=S2[:, tsl], start=True, stop=True),
                     reads=["S2all", "ones32"], writes=[("ps", s2k)])
                mt = meanA[:, tsl]
                rt = rstdA[:, tsl]
                P.op("dve", lambda e, s1=s1, mt=mt: e.tensor_single_scalar(out=mt, in_=s1, scalar=1.0 / D, op=ALU.mult),
                     reads=[("ps", s1k)], writes=[("mean", t4)])
                P.op("pool", lambda e, mt=mt, rt=rt: e.tensor_tensor(out=rt, in0=mt, in1=mt, op=ALU.mult),
                     reads=[("mean", t4)], writes=[("rstd6", t4)])
                P.op("dve", lambda e, s2=s2, rt=rt: e.scalar_tensor_tensor(out=rt, in0=s2, scalar=1.0 / D, in1=rt,
                                                                          op0=ALU.mult, op1=ALU.subtract),
                     reads=[("ps", s2k), ("rstd6", t4)], writes=[("rstd6", t4)])
                P.op("act", lambda e, rt=rt: e.activation(out=rt, in_=rt, func=AF.Sqrt, bias=epsb[:, :], scale=1.0),
                     reads=[("rstd6", t4), "epsb"], writes=[("rstd6", t4)])
                P.op("dve", lambda e, rt=rt: e.reciprocal(out=rt, in_=rt), reads=[("rstd6", t4)], writes=[("rstd6", t4)])

            P.barrier()
            W = Wz
            zitems = [(cc, t4) for cc in range(NCH) for t4 in range(4)]
            zacc = {}

            def z0(i):
                cc, t4 = zitems[i]
                a = tmp6[i % 6]
                P.op("dve", lambda e, a=a, cc=cc, t4=t4: e.tensor_tensor(
                    out=a, in0=ucT3[:, cc, t4 * 512:(t4 + 1) * 512], in1=meanA[:, t4 * 512:(t4 + 1) * 512], op=ALU.subtract),
                    reads=[("uc", cc, t4), ("mean", t4)], writes=[("tmp6", i % 6)])

            def z1(i):
                cc, t4 = zitems[i]
                a = tmp6[i % 6]
                P.op("pool", lambda e, a=a, t4=t4: e.tensor_tensor(out=a, in0=a, in1=rstdA[:, t4 * 512:(t4 + 1) * 512], op=ALU.mult),
                     reads=[("tmp6", i % 6), ("rstd6", t4)], writes=[("tmp6", i % 6)])

            def z2(i):
                cc, t4 = zitems[i]
                a = tmp6[i % 6]
                P.op("act", lambda e, a=a, cc=cc: e.activation(out=a, in_=a, func=AF.Silu,
                                                              bias=cvec[:, 2 * NCH + cc:2 * NCH + cc + 1],
                                                              scale=cvec[:, NCH + cc:NCH + cc + 1]),
                     reads=[("tmp6", i % 6), "cvec"], writes=[("tmp6", i % 6)])

            def z3(i):
                cc, t4 = zitems[i]
                wz, wzk = W.get(cc)
                acc, ak = PP[i % 4]

                def f(e, acc=acc, wz=wz, t4=t4):
                    ins = None
                    for kc in range(NCH):
                        ins = e.matmul(out=acc, lhsT=wz[:, kc, :], rhs=hT3[:, kc, t4 * 512:(t4 + 1) * 512],
                                       start=(kc == 0), stop=(kc == NCH - 1))
                    return ins
                P.op("pe", f, reads=[wzk] + hkeys(t4), writes=[("ps", ak)])
                zacc[i] = (acc, ak)

            def z4(i):
                acc, ak = zacc.pop(i)
                P.op("act", lambda e, acc=acc, i=i: e.activation(out=sg[i % 2][:, :], in_=acc, func=AF.Silu),
                     reads=[("ps", ak)], writes=[("sg", i % 2)])

            def z5(i):
                cc, t4 = zitems[i]
                a = tmp6[i % 6]
                P.op("dve", lambda e, a=a, i=i, cc=cc, t4=t4: e.tensor_tensor(
                    out=pT3[:, cc, t4 * 512:(t4 + 1) * 512], in0=a, in1=sg[i % 2][:, :], op=ALU.mult),
                    reads=[("tmp6", i % 6), ("sg", i % 2)], writes=[("pT", cc, t4)])

            run_pipeline([z0, z1, z2, z3, z4, z5], len(zitems))
            specs = []
            for c in range(NCH):
                specs.append(w_in[:, 8 * D + c * 128: 8 * D + (c + 1) * 128])
                specs.append(w_cv[:, c * 128:(c + 1) * 128])
            W6b = WStream(specs)
            W6b.ensure(NWS - 3)
            P.barrier()

            mT = uview(0, NCH * S, BF16)
            mT3 = mT.rearrange("p (c t) -> p c t", c=NCH)
            tA = uview(90112, 512, F32)
            tB = uview(90112 + 2048, 512, F32)
            W = W6b

            def pkeys(t4):
                return [("pT", c_, t4) for c_ in range(NCH)]
            k6 = 0
            for c in range(NCH):
                wg, wgk = W.get(2 * c)
                wy, wyk = W.get(2 * c + 1)
                for t4 in range(4):
                    accg, agk = proj_fm(wg, wgk, hT3, hkeys, t4)
                    accy, ayk = proj_fm(wy, wyk, pT3, pkeys, t4)
                    sgi = ev_i[0] % 2
                    ev_i[0] += 1
                    a = tA if k6 % 2 == 0 else tB
                    ak_ = "tA" if k6 % 2 == 0 else "tB"
                    k6 += 1
                    P.op("act", lambda e, accg=accg, sgi=sgi: e.activation(out=sg[sgi][:, :], in_=accg, func=AF.Sigmoid),
                         reads=[agk], writes=[("sg", sgi)])
                    P.op("dve", lambda e, a=a, accy=accy, sgi=sgi: e.tensor_tensor(out=a, in0=accy, in1=sg[sgi][:, :], op=ALU.mult),
                         reads=[ayk, ("sg", sgi)], writes=[ak_])
                    P.op("pool", lambda e, a=a, c=c, t4=t4: e.tensor_tensor(
                        out=a, in0=a, in1=m1T3[:, c, t4 * 512:(t4 + 1) * 512], op=ALU.add),
                        reads=[ak_, ("m1", c, t4)], writes=[ak_])
                    lo = S - (t4 + 1) * 512
                    P.op("dve", lambda e, a=a, c=c, lo=lo: e.tensor_copy(out=rev_free(mT3[:, c, lo:lo + 512]), in_=a),
                         reads=[ak_], writes=[("mT", c, 3 - t4)])
            wo = [uview(32768 + i * 8192, NCH * 512, BF16).rearrange("p (k n) -> p k n", k=NCH) for i in range(2)]
            NXT, NT1, NST = 6, 16, 5
            xts = [uview(49152 + i * 4096, 1024, F32) for i in range(NXT)]
            t1s = [R1[:, i * 2048:(i + 1) * 2048].bitcast(F32) for i in range(8)] + \
                  [R2[:, i * 2048:(i + 1) * 2048].bitcast(F32) for i in range(8)]
            g_in = uview(73728, 1024, F32)
            b_in = uview(73728 + 4096, 1024, F32)
            g_po = uview(73728 + 8192, 1024, F32)
            b_po = uview(73728 + 12288, 1024, F32)
            P.dma("sp", lambda e, g_in=g_in: e.dma_start(out=g_in, in_=ln_in_g.partition_broadcast(128)), writes=["gb"])
            P.dma("sp", lambda e, b_in=b_in: e.dma_start(out=b_in, in_=ln_in_b.partition_broadcast(128)), writes=["gb"])
            P.dma("sp", lambda e, g_po=g_po: e.dma_start(out=g_po, in_=ln_post_g.partition_broadcast(128)), writes=["gb"])
            P.dma("sp", lambda e, b_po=b_po: e.dma_start(out=b_po, in_=ln_post_b.partition_broadcast(128)), writes=["gb"])
            for i in range(2):
                P.dma("pool", lambda e, i=i, wo=wo: e.dma_start(
                    out=wo[i], in_=w_out[:, i * 512:(i + 1) * 512].rearrange("(k p) n -> p k n", p=128)),
                    writes=[("wo", i)])
            for tb in range(NXT):
                P.dma("sp", lambda e, xt=xts[tb], tb=tb, s=s: e.dma_start(out=xt, in_=x_d[s, tb * 128:(tb + 1) * 128, :]),
                      writes=[("xt", tb)])
            P.barrier()

            accs = {}

            junk = [uview(90112 + i * 2048, 1024, BF16) for i in range(2)]
            alphar = uview(94208, 128, F32)
            P.op("pool", lambda e, alphar=alphar: e.memset(alphar[0:1, :], ALPHA), writes=["alphar"])

            def r0(tb):
                if tb < NXT:
                    return
                xt = xts[tb % NXT]
                P.dma("sp", lambda e, xt=xt, tb=tb, s=s: e.dma_start(out=xt, in_=x_d[s, tb * 128:(tb + 1) * 128, :]),
                      writes=[("xt", tb % NXT)])

            def r1(tb):
                ln_norm(xts[tb % NXT], ("xt", tb % NXT), t1s[tb % NT1], ("t1", tb % NT1), 0,
                        rstdP[:, tb:tb + 1], "rstdP_ro", nmrP[:, tb:tb + 1], "nmrP_ro")

            def r2(tb):
                ln_mulg(t1s[tb % NT1], ("t1", tb % NT1), g_in)
                accs[tb] = []
                for half in range(2):
                    acc, ak = PP[(2 * tb + half) % 6]

                    def ffin(e, acc=acc, half=half, tb=tb, wo=wo, b_in=b_in, alphar=alphar):
                        for kc in range(NCH):
                            e.matmul(out=acc, lhsT=mT3[:, kc, tb * 128:(tb + 1) * 128], rhs=wo[half][:, kc, :],
                                     start=(kc == 0), stop=False)
                        return e.matmul(out=acc, lhsT=alphar[0:1, :], rhs=b_in[0:1, half * 512:(half + 1) * 512],
                                        start=False, stop=True)
                    P.op("pe", ffin, reads=[("mT", c_, tb // 4) for c_ in range(NCH)] + [("wo", half), "gb", "alphar"], writes=[("ps", ak)])
                    accs[tb].append((acc, ak))

            def r3(tb):
                pass

            def r4(tb):
                t1 = t1s[tb % NT1]
                t1k = ("t1", tb % NT1)
                for half, (acc, ak) in enumerate(accs.pop(tb)):
                    P.op("dve", lambda e, acc=acc, half=half, t1=t1: e.scalar_tensor_tensor(
                        out=t1[:, half * 512:(half + 1) * 512], in0=t1[:, half * 512:(half + 1) * 512], scalar=ALPHA,
                        in1=acc, op0=ALU.mult, op1=ALU.add),
                        reads=[("ps", ak), t1k], writes=[t1k])

            def r5(tb):
                b = NST + tb % NST
                t1 = t1s[tb % NT1]
                P.op("act", lambda e, t1=t1, b=b: e.activation(out=junk[0], in_=t1, func=AF.Identity, accum_out=mv[b][:, 0:1]),
                     reads=[("t1", tb % NT1)], writes=[("s1", b), "junk0"])
                P.op("act", lambda e, t1=t1, b=b: e.activation(out=junk[1], in_=t1, func=AF.Square, accum_out=mv[b][:, 1:2]),
                     reads=[("t1", tb % NT1)], writes=[("s2", b), "junk1"])

            def r6(tb):
                b = NST + tb % NST
                P.op("dve", lambda e, b=b: e.tensor_single_scalar(out=mv[b][:, 0:1], in_=mv[b][:, 0:1], scalar=1.0 / D, op=ALU.mult),
                     reads=[("s1", b)], writes=[("s1", b)])
                P.op("dve", lambda e, b=b: e.tensor_tensor(out=st6[b][:, 0:1], in0=mv[b][:, 0:1], in1=mv[b][:, 0:1], op=ALU.mult),
                     reads=[("s1", b)], writes=[("msq", b)])
                P.op("dve", lambda e, b=b: e.scalar_tensor_tensor(out=mv[b][:, 1:2], in0=mv[b][:, 1:2], scalar=1.0 / D,
                                                                 in1=st6[b][:, 0:1], op0=ALU.mult, op1=ALU.subtract),
                     reads=[("s2", b), ("msq", b)], writes=[("mv", b)])

            def r7(tb):
                ln_sqrt(NST + tb % NST)

            def r8(tb):
                b = NST + tb % NST
                P.op("dve", lambda e, b=b: e.reciprocal(out=rstd[b][:, :], in_=rstd[b][:, :]),
                     reads=[("rstd", b)], writes=[("rstd", b)])
                P.op("dve", lambda e, b=b: e.scalar_tensor_tensor(out=nmr[b][:, :], in0=mv[b][:, 0:1], scalar=-1.0,
                                                                 in1=rstd[b][:, :], op0=ALU.mult, op1=ALU.mult),
                     reads=[("s1", b), ("rstd", b)], writes=[("nmr", b)])

            def r9(tb):
                ln_norm(t1s[tb % NT1], ("t1", tb % NT1), t1s[tb % NT1], ("t1", tb % NT1), NST + tb % NST)

            def r10(tb):
                ln_mulg(t1s[tb % NT1], ("t1", tb % NT1), g_po)

            def r11(tb):
                t1 = t1s[tb % NT1]
                P.op("dve", lambda e, t1=t1: e.tensor_tensor(out=t1, in0=t1, in1=b_po, op=ALU.add),
                     reads=[("t1", tb % NT1), "gb"], writes=[("t1", tb % NT1)])

            def r12(tb):
                t1 = t1s[tb % NT1]
                P.dma("sp", lambda e, t1=t1, tb=tb, s=s: e.dma_start(out=y_d[s, tb * 128:(tb + 1) * 128, :], in_=t1),
                      reads=[("t1", tb % NT1)], writes=[("y", tb)])

            def r1011(tb):
                r10(tb)
                r11(tb)
            run_pipeline([r0, r1, r2, r4, r5, r6, r7, r8, r9, r1011, r12], NB)
            P.barrier()

        block = E(nc.Block())
        P.replay(block)
    return nc


_CACHE = {}


def kernel(**inputs):
    n = 8
    x = np.ascontiguousarray(inputs["x"], dtype=np.float32)
    shards = x.reshape(n, NSEQ, S, D)
    common = {
        "ln_in_g": np.ascontiguousarray(inputs["ln_in_g"], dtype=np.float32),
        "ln_in_b": np.ascontiguousarray(inputs["ln_in_b"], dtype=np.float32),
        "w_in": np.ascontiguousarray(inputs["w_in"][0], dtype=np.float32),
        "w_sb_proj": np.ascontiguousarray(inputs["w_sb_proj"][0], dtype=np.float32),
        "conv_w": np.ascontiguousarray(inputs["conv_w"][0], dtype=np.float32),
        "conv_b": np.ascontiguousarray(inputs["conv_b"][0], dtype=np.float32),
        "conv_ln_g": np.ascontiguousarray(inputs["conv_ln_g"][0], dtype=np.float32),
        "conv_ln_b": np.ascontiguousarray(inputs["conv_ln_b"][0], dtype=np.float32),
        "w_cv_proj": np.ascontiguousarray(inputs["w_cv_proj"][0], dtype=np.float32),
        "w_out": np.ascontiguousarray(inputs["w_out"][0], dtype=np.float32),
        "ln_post_g": np.ascontiguousarray(inputs["ln_post_g"][0], dtype=np.float32),
        "ln_post_b": np.ascontiguousarray(inputs["ln_post_b"][0], dtype=np.float32),
    }
    nc = build_program()
    in_maps = []
    for i in range(n):
        m = dict(common)
        m["x"] = np.ascontiguousarray(shards[i])
        in_maps.append(m)
    res = run_bass_kernel_spmd(nc, in_maps, core_ids=list(range(n)))
    out = np.concatenate([np.asarray(r["y"]) for r in res.results], axis=0)
    return out.reshape(16, S, D).astype(np.float32)
```

```python
import numpy as np
from contextlib import ExitStack
import concourse.bass as bass
import concourse.mybir as mybir
from concourse.bass_utils import run_bass_kernel_spmd

F32 = mybir.dt.float32
BF16 = mybir.dt.bfloat16
AF = mybir.ActivationFunctionType
ALU = mybir.AluOpType

S = 2048
D = 1024
NB = 16
NCH = 8
NSEQ = 2
EPS = 1e-5
ALPHA = 2.0 ** 0.25
ENG = ("pe", "act", "dve", "pool", "sp")
NDMA = 8
NWS = 5


def rev_free(ap):
    a = [list(d) for d in ap.ap]
    step, n = a[-1]
    off = ap.offset + step * (n - 1)
    a[-1] = [-step, n]
    return bass.AP(tensor=ap.tensor, offset=off, ap=a)


class Prog:
    def __init__(self, nc, es):
        self.nc = nc
        self.q = {e: [] for e in ENG}
        self.sems = {}
        self.cnt = {}
        for e in ENG:
            self.sems[e] = es.enter_context(nc.semaphore("s_" + e))
            self.cnt[e] = 0
        for qn in ("sp", "pool", "act"):
            for i in range(NDMA):
                k = ("dma", qn, i)
                self.sems[k] = es.enter_context(nc.semaphore("d_%s_%d" % (qn, i)))
                self.cnt[k] = 0
        self.dma_i = {"sp": 0, "pool": 0, "act": 0}
        self.lastw = {}
        self.readers = {}

    def _deps(self, reads, writes):
        waits = {}

        def need(ev):
            k, c = ev
            if waits.get(k, 0) < c:
                waits[k] = c

        for k in reads:
            if k in self.lastw:
                need(self.lastw[k])
        for k in writes:
            if k in self.lastw:
                need(self.lastw[k])
            for r in self.readers.get(k, ()):
                need(r)
        return waits

    def _commit(self, ev, reads, writes):
        for k in reads:
            self.readers.setdefault(k, []).append(ev)
        for k in writes:
            self.lastw[k] = ev
            self.readers[k] = []

    def op(self, eng, fn, reads=(), writes=()):
        waits = self._deps(reads, writes)
        self.cnt[eng] += 1
        ev = (eng, self.cnt[eng])
        self._commit(ev, reads, writes)
        self.q[eng].append((waits, fn, self.sems[eng], 1))
        return ev

    def dma(self, qn, fn, reads=(), writes=()):
        waits = self._deps(reads, writes)
        i = self.dma_i[qn] % NDMA
        self.dma_i[qn] += 1
        k = ("dma", qn, i)
        if self.cnt[k] > 0:
            if waits.get(k, 0) < self.cnt[k]:
                waits[k] = self.cnt[k]
        self.cnt[k] += 16
        ev = (k, self.cnt[k])
        self._commit(ev, reads, writes)
        self.q[qn].append((waits, fn, self.sems[k], 16))
        return ev

    def barrier(self):
        allk = {k: c for k, c in self.cnt.items() if c > 0}
        for e in ENG:
            self.q[e].append((dict(allk), None, None, 0))
        self.lastw = {}
        self.readers = {}

    def replay(self, block):
        prog = self

        def run(eng_name):
            def body(e):
                waited = {}
                for waits, fn, sem, inc in prog.q[eng_name]:
                    for k, c in waits.items():
                        if waited.get(k, 0) < c:
                            e.wait_ge(prog.sems[k], c)
                            waited[k] = c
                    if fn is not None:
                        ins = fn(e)
                        ins.then_inc(sem, inc)
            return body

        block.tensor(run("pe"))
        block.scalar(run("act"))
        block.vector(run("dve"))
        block.gpsimd(run("pool"))
        block.sync(run("sp"))


def build_program():
    nc = bass.Bass("TRN2", target_bir_lowering=False)
    x_d = nc.dram_tensor("x", [NSEQ, S, D], F32, kind="ExternalInput").ap()
    ln_in_g = nc.dram_tensor("ln_in_g", [D], F32, kind="ExternalInput").ap()
    ln_in_b = nc.dram_tensor("ln_in_b", [D], F32, kind="ExternalInput").ap()
    w_in = nc.dram_tensor("w_in", [D, 9 * D], F32, kind="ExternalInput").ap()
    w_sb = nc.dram_tensor("w_sb_proj", [D, D], F32, kind="ExternalInput").ap()
    conv_w = nc.dram_tensor("conv_w", [31, D], F32, kind="ExternalInput").ap()
    conv_b = nc.dram_tensor("conv_b", [D], F32, kind="ExternalInput").ap()
    cln_g = nc.dram_tensor("conv_ln_g", [D], F32, kind="ExternalInput").ap()
    cln_b = nc.dram_tensor("conv_ln_b", [D], F32, kind="ExternalInput").ap()
    w_cv = nc.dram_tensor("w_cv_proj", [D, D], F32, kind="ExternalInput").ap()
    w_out = nc.dram_tensor("w_out", [D, D], F32, kind="ExternalInput").ap()
    ln_post_g = nc.dram_tensor("ln_post_g", [D], F32, kind="ExternalInput").ap()
    ln_post_b = nc.dram_tensor("ln_post_b", [D], F32, kind="ExternalInput").ap()
    y_d = nc.dram_tensor("y", [NSEQ, S, D], F32, kind="ExternalOutput").ap()

    with ExitStack() as es:
        E = es.enter_context

        def sb(name, shape, dt):
            return E(nc.sbuf_tensor(name, shape, dt))

        identb = sb("identb", [128, 128], BF16)
        Jb = sb("Jb", [128, 128], BF16)
        ones32 = sb("ones32", [128, 128], F32)
        epsb = sb("epsb", [128, 1], F32)
        cw = sb("cw", [128, NCH * 31], F32)
        cvec = sb("cvec", [128, 3 * NCH], F32)
        hT = sb("hT", [128, NCH * S], BF16)
        R1 = sb("R1", [128, NCH * S], BF16)
        R2 = sb("R2", [128, NCH * S], BF16)
        ws = [sb("ws%d" % i, [128, NCH * 128], BF16) for i in range(NWS)]
        sg = [sb("sg%d" % i, [128, 512], F32) for i in range(2)]
        st6 = [sb("st6_%d" % i, [128, 12], F32) for i in range(10)]
        mv = [sb("mv%d" % i, [128, 2], F32) for i in range(10)]
        rstd = [sb("rstd%d" % i, [128, 1], F32) for i in range(10)]
        nmr = [sb("nmr%d" % i, [128, 1], F32) for i in range(10)]
        rstdP = sb("rstdP", [128, NB], F32)
        nmrP = sb("nmrP", [128, NB], F32)
        UB = 96256
        U = sb("U", [128, UB // 2], BF16)

        def uview(off, nelem, dt):
            if dt == BF16:
                return U[:, off // 2: off // 2 + nelem]
            return U[:, off // 2: off // 2 + 2 * nelem].bitcast(F32)

        lg = [E(nc.psum_tensor("lg%d" % i, [128, 1024], F32)) for i in range(2)]
        pt = [E(nc.psum_tensor("pt%d" % i, [128, 1024], BF16)) for i in range(2)]
        po = [E(nc.psum_tensor("po%d" % i, [128, 512], F32)) for i in range(2)]
        PP = [(lg[0][:, 0:512], "lg0a"), (lg[0][:, 512:1024], "lg0b"),
              (lg[1][:, 0:512], "lg1a"), (lg[1][:, 512:1024], "lg1b"),
              (po[0][:, :], "po0"), (po[1][:, :], "po1")]

        P = Prog(nc, es)
        hT3 = hT[:, :].rearrange("p (c t) -> p c t", c=NCH)
        vrev3 = R1[:, :].rearrange("p (b f) -> p b f", b=NB)
        m1T3 = R1[:, :].rearrange("p (c t) -> p c t", c=NCH)
        gT3 = R2[:, :].rearrange("p (c t) -> p c t", c=NCH)
        ws3 = [w[:, :].rearrange("p (k n) -> p k n", k=NCH) for w in ws]
        cw3 = cw[:, :].rearrange("p (c j) -> p c j", c=NCH)

        io1 = uview(94800, 128, F32)
        io2 = uview(94800 + 512, 128, F32)
        P.op("pool", lambda e: e.iota(io1, pattern=[[1, 128]], base=0, channel_multiplier=-1,
                                      allow_small_or_imprecise_dtypes=True), writes=["io1"])
        P.op("pool", lambda e: e.iota(io2, pattern=[[1, 128]], base=-127, channel_multiplier=1,
                                      allow_small_or_imprecise_dtypes=True), writes=["io2"])
        P.op("pool", lambda e: e.memset(ones32[:, :], 1.0), writes=["ones32"])
        P.op("pool", lambda e: e.memset(epsb[:, :], EPS), writes=["epsb"])
        P.op("dve", lambda e: e.tensor_single_scalar(out=identb[:, :], in_=io1, scalar=0.0, op=ALU.is_equal),
             reads=["io1"], writes=["identb"])
        P.op("dve", lambda e: e.tensor_single_scalar(out=Jb[:, :], in_=io2, scalar=0.0, op=ALU.is_equal),
             reads=["io2"], writes=["Jb"])
        cwrow = uview(55296, 1024, F32)
        cvrow = uview(59392, 1024, F32)
        ident32 = uview(63488, 128, F32)
        P.op("dve", lambda e: e.tensor_single_scalar(out=ident32, in_=io1, scalar=0.0, op=ALU.is_equal),
             reads=["io1"], writes=["ident32"])
        P.dma("sp", lambda e: e.dma_start(out=cwrow[0:31, :], in_=conv_w), writes=["cwrow"])
        for i, v in enumerate((conv_b, cln_g, cln_b)):
            P.dma("sp", lambda e, i=i, v=v: e.dma_start(out=cvrow[i:i + 1, :], in_=v.rearrange("(o n) -> o n", o=1)),
                  writes=[("cvrow", i)])

        def fcw(e):
            ins = None
            for cc in range(NCH):
                ins = e.matmul(out=po[0][:, cc * 32: cc * 32 + 31], lhsT=cwrow[0:31, cc * 128:(cc + 1) * 128],
                               rhs=ident32[0:31, 0:31], start=True, stop=True)
            return ins
        P.op("pe", fcw, reads=["cwrow", "ident32"], writes=[("ps", "po0")])
        P.op("dve", lambda e: e.tensor_copy(out=cw3, in_=po[0][:, 0:256].rearrange("p (c j) -> p c j", c=NCH)[:, :, 0:31]),
             reads=[("ps", "po0")], writes=["cw"])

        def fcv0(e):
            ins = None
            for cc in range(NCH):
                ins = e.matmul(out=po[1][:, cc * 4: cc * 4 + 3], lhsT=cvrow[0:3, cc * 128:(cc + 1) * 128],
                               rhs=ident32[0:3, 0:3], start=True, stop=True)
            return ins
        P.op("pe", fcv0, reads=[("cvrow", 0), ("cvrow", 1), ("cvrow", 2), "ident32"], writes=[("ps", "po1")])
        P.op("dve", lambda e: e.tensor_copy(
            out=cvec[:, :].rearrange("p (i c) -> p i c", i=3),
            in_=po[1][:, 0:32].rearrange("p (c i) -> p c i", c=NCH)[:, :, 0:3].rearrange("p c i -> p i c")),
            reads=[("ps", "po1")], writes=["cvec"])
        P.barrier()

        class WStream:
            def __init__(self, specs):
                self.specs = specs
                self.loaded = 0
                self.base = WStream.gbase
                WStream.gbase += len(specs)

            def ensure(self, upto):
                while self.loaded < min(upto + 1, len(self.specs)):
                    i = self.loaded
                    slot = (self.base + i) % NWS
                    src = self.specs[i].rearrange("(k p) n -> p k n", p=128)
                    P.dma("pool", lambda e, slot=slot, src=src: e.dma_start(out=ws3[slot], in_=src),
                          writes=[("ws", slot)])
                    self.loaded += 1

            def get(self, i):
                self.ensure(i + NWS - 2)
                slot = (self.base + i) % NWS
                return ws3[slot], ("ws", slot)

        WStream.gbase = 0
        acc_i = [0]
        ev_i = [0]

        def next_acc(n=len(PP)):
            a = PP[acc_i[0] % n]
            acc_i[0] += 1
            return a

        def ln_stats(xt, xk, b):
            P.op("dve", lambda e: e.bn_stats(out=st6[b][:, 0:6], in_=xt[:, 0:512]), reads=[xk], writes=[("st6a", b)])
            P.op("dve", lambda e: e.bn_stats(out=st6[b][:, 6:12], in_=xt[:, 512:1024]), reads=[xk], writes=[("st6b", b)])
            P.op("dve", lambda e: e.bn_aggr(out=mv[b][:, :], in_=st6[b][:, :]),
                 reads=[("st6a", b), ("st6b", b)], writes=[("mv", b)])

        def ln_sqrt(b, rs=None, rk=None):
            rs = rstd[b][:, :] if rs is None else rs
            rk = ("rstd", b) if rk is None else rk
            P.op("act", lambda e: e.activation(out=rs, in_=mv[b][:, 1:2], func=AF.Sqrt,
                                               bias=epsb[:, :], scale=1.0),
                 reads=[("mv", b), "epsb"], writes=[rk])

        def ln_recip(b, rs=None, rk=None, nm=None, nk=None):
            rs = rstd[b][:, :] if rs is None else rs
            rk = ("rstd", b) if rk is None else rk
            nm = nmr[b][:, :] if nm is None else nm
            nk = ("nmr", b) if nk is None else nk
            P.op("dve", lambda e: e.reciprocal(out=rs, in_=rs), reads=[rk], writes=[rk])
            P.op("dve", lambda e: e.scalar_tensor_tensor(out=nm, in0=mv[b][:, 0:1], scalar=-1.0,
                                                         in1=rs, op0=ALU.mult, op1=ALU.mult),
                 reads=[("mv", b), rk], writes=[nk])

        def ln_norm(xt, xk, t1, t1k, b, rs=None, rk=None, nm=None, nk=None):
            rs = rstd[b][:, :] if rs is None else rs
            rk = ("rstd", b) if rk is None else rk
            nm = nmr[b][:, :] if nm is None else nm
            nk = ("nmr", b) if nk is None else nk
            P.op("act", lambda e: e.activation(out=t1, in_=xt, func=AF.Identity, bias=nm, scale=rs),
                 reads=[xk, rk, nk], writes=[t1k])

        def ln_mulg(t1, t1k, gbc):
            P.op("dve", lambda e: e.tensor_tensor(out=t1, in0=t1, in1=gbc, op=ALU.mult),
                 reads=[t1k, "gb"], writes=[t1k])

        def ln_addb(t1, t1k, bbc, out_ap, out_key):
            P.op("pool", lambda e: e.tensor_tensor(out=out_ap, in0=t1, in1=bbc, op=ALU.add),
                 reads=[t1k, "gb"], writes=[out_key])

        def run_pipeline(stages, nblocks):
            for k in range(nblocks + len(stages) - 1):
                for si, fn_ in enumerate(stages):
                    if 0 <= k - si < nblocks:
                        fn_(k - si)

        def proj_fm(wt, wk, rhs3, rkeyf, t4):
            acc, ak = next_acc()

            def f(e, acc=acc, wt=wt, t4=t4):
                ins = None
                for kc in range(NCH):
                    ins = e.matmul(out=acc, lhsT=wt[:, kc, :], rhs=rhs3[:, kc, t4 * 512:(t4 + 1) * 512],
                                   start=(kc == 0), stop=(kc == NCH - 1))
                return ins
            P.op("pe", f, reads=[wk] + rkeyf(t4), writes=[("ps", ak)])
            return acc, ("ps", ak)

        def hkeys(t4):
            return [("hT", t4 * 4 + j) for j in range(4)]

        for s in range(NSEQ):
            NXT, NT1, NHB, NST = 5, 4, 3, 5
            xts = [uview(i * 4096, 1024, F32) for i in range(NXT)]
            t1s = [uview(20480 + i * 4096, 1024, F32) for i in range(NT1)]
            hbs = [uview(36864 + i * 2048, 1024, BF16) for i in range(NHB)]
            g_in = uview(43008, 1024, F32)
            b_in = uview(47104, 1024, F32)
            P.dma("sp", lambda e, g_in=g_in: e.dma_start(out=g_in, in_=ln_in_g.partition_broadcast(128)), writes=["gb"])
            P.dma("sp", lambda e, b_in=b_in: e.dma_start(out=b_in, in_=ln_in_b.partition_broadcast(128)), writes=["gb"])

            def q0(tb):
                xt = xts[tb % NXT]
                P.dma("sp", lambda e, xt=xt, tb=tb, s=s: e.dma_start(out=xt, in_=x_d[s, tb * 128:(tb + 1) * 128, :]),
                      writes=[("xt", tb % NXT)])

            def q1(tb):
                ln_stats(xts[tb % NXT], ("xt", tb % NXT), tb % NST)

            def q2(tb):
                ln_sqrt(tb % NST, rstdP[:, tb:tb + 1], ("rstdP", tb))

            def q3(tb):
                ln_recip(tb % NST, rstdP[:, tb:tb + 1], ("rstdP", tb), nmrP[:, tb:tb + 1], ("nmrP", tb))

            def q4(tb):
                ln_norm(xts[tb % NXT], ("xt", tb % NXT), t1s[tb % NT1], ("t1", tb % NT1), tb % NST,
                        rstdP[:, tb:tb + 1], ("rstdP", tb), nmrP[:, tb:tb + 1], ("nmrP", tb))

            def q5(tb):
                ln_mulg(t1s[tb % NT1], ("t1", tb % NT1), g_in)

            def q6(tb):
                ln_addb(t1s[tb % NT1], ("t1", tb % NT1), b_in, hbs[tb % NHB], ("hb", tb % NHB))

            def q7(tb):
                b = tb % 2
                hb = hbs[tb % NHB]

                def ftr(e, hb=hb, b=b):
                    ins = None
                    for fc in range(NCH):
                        ins = e.matmul(out=lg[b][:, fc * 128:(fc + 1) * 128], lhsT=hb[:, fc * 128:(fc + 1) * 128],
                                       rhs=Jb[:, :], start=True, stop=True)
                    return ins
                P.op("pe", ftr, reads=[("hb", tb % NHB), "Jb"], writes=[("ps", "lg%da" % b), ("ps", "lg%db" % b)])

            def q8(tb):
                b = tb % 2
                tr = NB - 1 - tb
                P.op("act", lambda e, b=b, tr=tr: e.activation(out=hT3[:, :, tr * 128:(tr + 1) * 128],
                                                              in_=lg[b][:, :].rearrange("p (c t) -> p c t", c=NCH), func=AF.Copy),
                     reads=[("ps", "lg%da" % b), ("ps", "lg%db" % b)], writes=[("hT", tr)])

            Wv = WStream([w_in[:, 2 * D + c * 128: 2 * D + (c + 1) * 128] for _g in range(4) for c in range(NCH)])
            vcnt = [0]

            vitems = [(g4, c) for g4 in (3, 2, 1, 0) for c in range(NCH)]
            vnext = [0]

            def emit_v(i):
                g4, c = vitems[i]
                wt, wk = Wv.get(i)
                acc, ak = PP[4 + i % 2]

                def fv(e, acc=acc, wt=wt, g4=g4):
                    ins = None
                    for j in range(4):
                        tb_ = g4 * 4 + j
                        for kc in range(NCH):
                            ins = e.matmul(out=acc[:, j * 128:(j + 1) * 128], lhsT=hT3[:, kc, tb_ * 128:(tb_ + 1) * 128],
                                           rhs=wt[:, kc, :], start=(kc == 0), stop=(kc == NCH - 1))
                    return ins
                P.op("pe", fv, reads=[wk] + hkeys(g4), writes=[("ps", ak)])
                dst = vrev3[:, g4 * 4:(g4 + 1) * 4, c * 128:(c + 1) * 128]
                src = acc.rearrange("p (j f) -> p j f", j=4)
                P.op("act", lambda e, dst=dst, src=src: e.activation(out=dst, in_=src, func=AF.Copy),
                     reads=[("ps", ak)], writes=[("v", c, g4)])

            def q9(tb):
                for _ in range(2):
                    i = vnext[0]
                    if i >= len(vitems):
                        return
                    g4, _c = vitems[i]
                    if NB - 1 - 4 * g4 > tb:
                        return
                    emit_v(i)
                    vnext[0] += 1

            run_pipeline([q0, q1, q2, q3, q4, q5, q6, q7, q8, q9], NB)
            while vnext[0] < len(vitems):
                emit_v(vnext[0])
                vnext[0] += 1
            specs = []
            for c in range(NCH):
                for goff in (0, D, 3 * D):
                    specs.append(w_in[:, goff + c * 128: goff + (c + 1) * 128])
            W = WStream(specs)
            W.ensure(NWS - 3)
            P.barrier()

            qT = [uview(i * 4096, S, BF16) for i in range(2)]
            kT = [uview(8192 + i * 4096, S, BF16) for i in range(2)]
            szT = [uview(16384 + i * 4096, S, BF16) for i in range(2)]
            NPIX = 4
            PiX = [uview(24576 + i * 8208, S + 1, F32) for i in range(NPIX)]
            NWB = 4
            wb = [uview(57408 + i * 4096, S, BF16) for i in range(NWB)]
            NWT = 8
            wT = [uview(73792 + i * 2048, 1024, BF16) for i in range(NWT)]
            negmask = uview(94288, 128, BF16)
            P.op("pool", lambda e: e.iota(io1, pattern=[[1, 128]], base=0, channel_multiplier=-1,
                                          allow_small_or_imprecise_dtypes=True), writes=["io1"])
            P.op("dve", lambda e, negmask=negmask: e.tensor_scalar(out=negmask, in0=io1, scalar1=0.0, scalar2=-30000.0,
                                                                  op0=ALU.is_le, op1=ALU.mult),
                 reads=["io1"], writes=["maskd"])
            for i in range(NPIX):
                P.op("pool", lambda e, i=i, PiX=PiX: e.memset(PiX[i][:, 0:1], 1.0), writes=[("pix0", i)])

            NLB = 2
            cnt = {"piece": 0, "grp": 0}
            czero = po[1][:, 0:1]
            P.op("dve", lambda e: e.memset(czero, 0.0), writes=["czero"])

            def emit_proj(c, gi, t4):
                cb = c % 2
                wt, wk = W.get(c * 3 + gi)
                dstT = (qT[cb], kT[cb], szT[cb])[gi]
                sl_ = cnt["piece"] % NLB
                cnt["piece"] += 1
                pacc = lg[sl_][:, 0:512]
                pacck = ("ps", "lg%da" % sl_)

                def f(e, wt=wt, t4=t4, pacc=pacc):
                    ins = None
                    for kc in range(NCH):
                        ins = e.matmul(out=pacc, lhsT=wt[:, kc, :], rhs=hT3[:, kc, t4 * 512:(t4 + 1) * 512],
                                       start=(kc == 0), stop=(kc == NCH - 1))
                    return ins
                P.op("pe", f, reads=[wk] + hkeys(t4), writes=[pacck, ("ps", "lg%db" % sl_)])
                dst = dstT[:, t4 * 512:(t4 + 1) * 512]
                dk = (("q", "k", "sz")[gi], cb, t4)
                if gi == 0:
                    P.op("act", lambda e, dst=dst, pacc=pacc: e.activation(out=dst, in_=pacc, func=AF.Copy),
                         reads=[pacck], writes=[dk])
                elif gi == 1:
                    P.op("act", lambda e, dst=dst, pacc=pacc: e.activation(out=dst, in_=pacc, func=AF.Copy),
                         reads=[pacck], writes=[dk])
                else:
                    sgi = ev_i[0] % 2
                    ev_i[0] += 1
                    P.op("act", lambda e, sgi=sgi, pacc=pacc: e.activation(out=sg[sgi][:, :], in_=pacc, func=AF.Sigmoid),
                         reads=[pacck], writes=[("sg", sgi)])
                    P.op("dve", lambda e, dst=dst, sgi=sgi, pacc=pacc: e.tensor_tensor(out=dst, in0=pacc, in1=sg[sgi][:, :], op=ALU.mult),
                         reads=[pacck, ("sg", sgi)], writes=[dk])

            proj_list = [(gi, t4) for gi in range(3) for t4 in range(4)]
            for gi, t4 in proj_list:
                emit_proj(0, gi, t4)

            items = [(c, qb, hh) for c in range(NCH) for qb in range(NB) for hh in range(2)]
            NI = len(items)
            info = {}

            def S1(n):
                c, qb, hh = items[n]
                cb = c % 2
                col0 = qb * 128
                L = (NB - qb) * 128
                ib = n % NPIX
                hs = slice(hh * 64, (hh + 1) * 64)
                npiece = (L + 1023) // 1024
                for p_ in range(npiece):
                    sl_ = cnt["piece"] % NLB
                    cnt["piece"] += 1
                    c0 = col0 + p_ * 1024
                    m = min(1024, L - p_ * 1024)
                    lk = [("ps", "lg%da" % sl_), ("ps", "lg%db" % sl_)]

                    def fqk(e, sl_=sl_, hs=hs, col0=col0, cb=cb, c0=c0, m=m, p_=p_):
                        ins = None
                        for sub in range((m + 511) // 512):
                            mm = min(512, m - sub * 512)
                            lo_ = 0
                            if p_ == 0 and sub == 0:
                                e.matmul(out=lg[sl_][:, 0:128], lhsT=qT[cb][hs, col0:col0 + 128],
                                         rhs=kT[cb][hs, c0:c0 + 128], start=True, stop=False)
                                ins = e.matmul(out=lg[sl_][:, 0:128], lhsT=identb[:, :], rhs=negmask[:, :], start=False, stop=True)
                                lo_ = 128
                            if mm > lo_:
                                ins = e.matmul(out=lg[sl_][:, sub * 512 + lo_: sub * 512 + mm], lhsT=qT[cb][hs, col0:col0 + 128],
                                               rhs=kT[cb][hs, c0 + sub * 512 + lo_: c0 + sub * 512 + mm], start=True, stop=True)
                        return ins
                    kk = sorted(set(("k", cb, (c0 + o) // 512) for o in range(0, m, 128)))
                    P.op("pe", fqk, reads=[("q", cb, qb // 4), "maskd", "identb"] + kk, writes=lk)
                    P.op("act", lambda e, sl_=sl_, p_=p_, ib=ib, m=m: e.activation(
                        out=PiX[ib][:, 1 + p_ * 1024: 1 + p_ * 1024 + m], in_=lg[sl_][:, 0:m], func=AF.Sigmoid, scale=-0.125),
                        reads=lk, writes=[("pix", ib, p_)])

            def S23(n):
                c, qb, hh = items[n]
                L = (NB - qb) * 128
                ib = n % NPIX
                iw = n % NWB
                pixk = [("pix", ib, p_) for p_ in range(2)]
                zb = bass.AP(tensor=czero.tensor, offset=czero.offset, ap=[list(czero.ap[0]), [0, L]])
                P.op("dve", lambda e, ib=ib, L=L, zb=zb: e.tensor_tensor_scan(
                    out=PiX[ib][:, 1:L + 1], data0=PiX[ib][:, 1:L + 1], data1=zb,
                    initial=1.0, op0=ALU.mult, op1=ALU.max),
                    reads=pixk + ["czero"], writes=pixk)
                P.op("pool", lambda e, ib=ib, iw=iw, L=L: e.tensor_tensor(
                    out=wb[iw][:, 0:L], in0=PiX[ib][:, 0:L], in1=PiX[ib][:, 1:L + 1], op=ALU.subtract),
                    reads=pixk + [("pix0", ib)], writes=[("wb", iw)])

            def S4(n):
                c, qb, hh = items[n]
                nblk = NB - qb
                iw = n % NWB
                groups = []
                for g in range((nblk + 7) // 8):
                    nbg = min(8, nblk - g * 8)
                    gp_ = cnt["grp"] % 2
                    gw = cnt["grp"] % NWT
                    gi_ = cnt["grp"]
                    cnt["grp"] += 1
                    groups.append((g, nbg, gw))

                    def ft(e, iw=iw, g=g, nbg=nbg, gp_=gp_):
                        ins = None
                        for j in range(nbg):
                            ins = e.transpose(out=pt[gp_][:, j * 128:(j + 1) * 128],
                                              in_=wb[iw][:, (g * 8 + j) * 128:(g * 8 + j + 1) * 128], identity=identb[:, :])
                        return ins
                    P.op("pe", ft, reads=[("wb", iw), "identb"], writes=[("ps", "pt%d" % gp_)])
                    if True:
                        P.op("act", lambda e, gw=gw, gp_=gp_, nbg=nbg: e.activation(
                            out=wT[gw][:, 0:nbg * 128], in_=pt[gp_][:, 0:nbg * 128], func=AF.Copy),
                            reads=[("ps", "pt%d" % gp_)], writes=[("wT", gw)])
                    else:
                        P.op("dve", lambda e, gw=gw, gp_=gp_, nbg=nbg: e.tensor_copy(
                            out=wT[gw][:, 0:nbg * 128], in_=pt[gp_][:, 0:nbg * 128]),
                            reads=[("ps", "pt%d" % gp_)], writes=[("wT", gw)])
                info[n] = groups

            def S5(n):
                c, qb, hh = items[n]
                assert hh == 1
                cb = c % 2
                col0 = qb * 128
                nblk = NB - qb
                g0 = info.pop(n - 1)
                g1 = info.pop(n)
                for (g, nbg, gw0), (_, _, gw1) in zip(g0, g1):
                    def fpv(e, gw0=gw0, gw1=gw1, g=g, nbg=nbg, qb=qb, c=c, nblk=nblk):
                        ins = None
                        for j in range(nbg):
                            kb = qb + g * 8 + j
                            first = (g == 0 and j == 0)
                            last = (g * 8 + j == nblk - 1)
                            e.matmul(out=po[0][0:64, 0:128], lhsT=vrev3[:, kb, c * 128: c * 128 + 64],
                                     rhs=wT[gw0][:, j * 128:(j + 1) * 128], start=first, stop=last)
                            ins = e.matmul(out=po[0][64:128, 0:128], lhsT=vrev3[:, kb, c * 128 + 64: c * 128 + 128],
                                           rhs=wT[gw1][:, j * 128:(j + 1) * 128], start=first, stop=last,
                                           tile_position=(0, 64))
                        return ins
                    vkeys = sorted(set(("v", c, (qb + g * 8 + j) // 4) for j in range(nbg)))
                    P.op("pe", fpv, reads=[("wT", gw0), ("wT", gw1)] + vkeys, writes=[("ps", "po0")])
                P.op("dve", lambda e, c=c, col0=col0, cb=cb: e.tensor_tensor(
                    out=gT3[:, c, col0:col0 + 128], in0=po[0][:, 0:128], in1=szT[cb][:, col0:col0 + 128], op=ALU.mult),
                    reads=[("ps", "po0"), ("sz", cb, qb // 4)], writes=[("gT", c, qb // 4)])

            LAG_T = 3
            LAG_PV = 5
            for n in range(NI + LAG_PV):
                if n < NI:
                    c, qb, hh = items[n]
                    if c + 1 < NCH and hh == 0 and 2 <= qb < 2 + len(proj_list):
                        gi, t4 = proj_list[qb - 2]
                        emit_proj(c + 1, gi, t4)
                    S1(n)
                if 0 <= n - LAG_T < NI:
                    S4(n - LAG_T)
                if 0 <= n - LAG_PV < NI and (n - LAG_PV) % 2 == 1:
                    S5(n - LAG_PV)
                if n < NI:
                    S23(n)
            specs = []
            for c in range(NCH):
                specs.append(w_in[:, 7 * D + c * 128: 7 * D + (c + 1) * 128])
                specs.append(w_sb[:, c * 128:(c + 1) * 128])
            W = WStream(specs)
            W.ensure(NWS - 3)
            P.barrier()


            def gkeys(t4):
                return [("gT", c_, t4) for c_ in range(NCH)]
            for c in range(NCH):
                wg, wgk = W.get(2 * c)
                wy, wyk = W.get(2 * c + 1)
                for t4 in range(4):
                    accg, agk = proj_fm(wg, wgk, hT3, hkeys, t4)
                    accy, ayk = proj_fm(wy, wyk, gT3, gkeys, t4)
                    sgi = ev_i[0] % 2
                    ev_i[0] += 1
                    P.op("act", lambda e, accg=accg, sgi=sgi: e.activation(out=sg[sgi][:, :], in_=accg, func=AF.Sigmoid),
                         reads=[agk], writes=[("sg", sgi)])
                    P.op("dve", lambda e, accy=accy, sgi=sgi, c=c, t4=t4: e.tensor_tensor(
                        out=m1T3[:, c, t4 * 512:(t4 + 1) * 512], in0=accy, in1=sg[sgi][:, :], op=ALU.mult),
                        reads=[ayk, ("sg", sgi)], writes=[("m1", c, t4)])

            ucT = uview(0, NCH * S, F32)
            ucT3 = ucT.rearrange("p (c t) -> p c t", c=NCH)
            uT = [uview(65536 + i * 4160, S + 32, BF16) for i in range(2)]
            Dg = uview(73856, 31 * 128, BF16)
            Dg3 = Dg.rearrange("p (j n) -> p j n", j=31)
            meanT = uview(81792, 512, F32)
            rstdT = uview(81792 + 2048, 512, F32)
            tA = uview(81792 + 4096, 512, F32)
            tB = uview(81792 + 6144, 512, F32)
            pT3 = gT3
            S1 = R2[:, 8192:12288].bitcast(F32)
            S2 = R2[:, 12288:16384].bitcast(F32)
            sqt = [R2[:, 6144 + i * 1024: 6144 + (i + 1) * 1024].bitcast(F32) for i in range(2)]
            for i in range(2):
                P.op("pool", lambda e, i=i: e.memset(uT[i][:, S:S + 32], 0.0), writes=[("upad", i)])
            specs = []
            for c in range(NCH):
                specs.append(w_in[:, 4 * D + c * 128: 4 * D + (c + 1) * 128])
                specs.append(w_in[:, 5 * D + c * 128: 5 * D + (c + 1) * 128])
            W = WStream(specs)
            def projU(cc):
                ub = cc % 2
                wv_, wvk = W.get(2 * cc)
                wg, wgk = W.get(2 * cc + 1)
                for t4 in range(4):
                    accv, avk = proj_fm(wv_, wvk, hT3, hkeys, t4)
                    accg, agk = proj_fm(wg, wgk, hT3, hkeys, t4)
                    sgi = ev_i[0] % 2
                    ev_i[0] += 1
                    P.op("act", lambda e, accg=accg, sgi=sgi: e.activation(out=sg[sgi][:, :], in_=accg, func=AF.Sigmoid),
                         reads=[agk], writes=[("sg", sgi)])
                    P.op("dve", lambda e, accv=accv, sgi=sgi, ub=ub, t4=t4: e.tensor_tensor(
                        out=uT[ub][:, t4 * 512:(t4 + 1) * 512], in0=accv, in1=sg[sgi][:, :], op=ALU.mult),
                        reads=[avk, ("sg", sgi)], writes=[("uT", ub, t4)])

            def buildDg(cc):
                ida = identb[:, :]
                idb3 = bass.AP(tensor=ida.tensor, offset=ida.offset, ap=[list(ida.ap[0]), [0, 31], list(ida.ap[1])])
                cwa = cw3[:, cc, :]
                cwb3 = bass.AP(tensor=cwa.tensor, offset=cwa.offset, ap=[list(cwa.ap[0]), list(cwa.ap[1]), [0, 128]])
                P.op("dve", lambda e, idb3=idb3, cwb3=cwb3: e.tensor_tensor(out=Dg3, in0=idb3, in1=cwb3, op=ALU.mult),
                     reads=["identb", "cw"], writes=["Dg"])

            def convU(cc):
                ub = cc % 2
                for t4 in range(4):
                    acc, ak = next_acc()

                    def fcv(e, acc=acc, ub=ub, t4=t4):
                        ins = None
                        for j in range(31):
                            o = t4 * 512 + 30 - j
                            ins = e.matmul(out=acc, lhsT=Dg3[:, j, :], rhs=uT[ub][:, o:o + 512], start=(j == 0), stop=(j == 30))
                        return ins
                    uk = [("uT", ub, t4), ("upad", ub)] + ([("uT", ub, t4 + 1)] if t4 < 3 else [])
                    P.op("pe", fcv, reads=uk + ["Dg"], writes=[("ps", ak)])
                    P.op("act", lambda e, acc=acc, cc=cc, t4=t4: e.activation(
                        out=ucT3[:, cc, t4 * 512:(t4 + 1) * 512], in_=acc, func=AF.Identity,
                        bias=cvec[:, cc:cc + 1], scale=1.0),
                        reads=[("ps", ak), "cvec"], writes=[("uc", cc, t4)])
                    tsl = slice(t4 * 512, (t4 + 1) * 512)
                    ucv = ucT3[:, cc, tsl]
                    if cc == 0:
                        P.op("dve", lambda e, ucv=ucv, tsl=tsl: e.tensor_copy(out=S1[:, tsl], in_=ucv),
                             reads=[("uc", cc, t4)], writes=[("S1", t4)])
                        P.op("dve", lambda e, ucv=ucv, tsl=tsl: e.tensor_tensor(out=S2[:, tsl], in0=ucv, in1=ucv, op=ALU.mult),
                             reads=[("uc", cc, t4)], writes=[("S2", t4)])
                    else:
                        qi = (cc * 4 + t4) % 2
                        P.op("dve", lambda e, ucv=ucv, tsl=tsl: e.tensor_tensor(out=S1[:, tsl], in0=S1[:, tsl], in1=ucv, op=ALU.add),
                             reads=[("uc", cc, t4), ("S1", t4)], writes=[("S1", t4)])
                        P.op("dve", lambda e, ucv=ucv, qi=qi: e.tensor_tensor(out=sqt[qi], in0=ucv, in1=ucv, op=ALU.mult),
                             reads=[("uc", cc, t4)], writes=[("sqt", qi)])
                        P.op("dve", lambda e, tsl=tsl, qi=qi: e.tensor_tensor(out=S2[:, tsl], in0=S2[:, tsl], in1=sqt[qi], op=ALU.add),
                             reads=[("sqt", qi), ("S2", t4)], writes=[("S2", t4)])

            buildDg(0)
            projU(0)
            for cc in range(NCH):
                if cc + 1 < NCH:
                    projU(cc + 1)
                convU(cc)
                if cc + 1 < NCH:
                    buildDg(cc + 1)
            Wz = WStream([w_in[:, 6 * D + cc * 128: 6 * D + (cc + 1) * 128] for cc in range(NCH)])
            Wz.ensure(NWS - 3)
            P.barrier()
            meanA = uview(65536, S, F32)
            rstdA = uview(65536 + 8192, S, F32)
            tmp6 = [uview(81920 + i * 2048, 512, F32) for i in range(6)]
            sqeng = ("act", "pool", "dve")
            for t4 in range(4):
                tsl = slice(t4 * 512, (t4 + 1) * 512)
                s1, s1k = PP[(2 * t4) % 6]
                s2, s2k = PP[(2 * t4 + 1) % 6]
                P.op("pe", lambda e, s1=s1, tsl=tsl: e.matmul(out=s1, lhsT=ones32[:, :], rhs=S1[:, tsl], start=True, stop=True),
                     reads=["S1all", "ones32"], writes=[("ps", s1k)])
                P.op("pe", lambda e, s2=s2, tsl=tsl: e.matmul(out=s2, lhsT=ones32[:, :], rhs=S2[:, tsl], start=True, stop=True),
                     reads=["S2all", "ones32"], writes=[("ps", s2k)])
                mt = meanA[:, tsl]
                rt = rstdA[:, tsl]
                P.op("dve", lambda e, s1=s1, mt=mt: e.tensor_single_scalar(out=mt, in_=s1, scalar=1.0 / D, op=ALU.mult),
                     reads=[("ps", s1k)], writes=[("mean", t4)])
                P.op("pool", lambda e, mt=mt, rt=rt: e.tensor_tensor(out=rt, in0=mt, in1=mt, op=ALU.mult),
                     reads=[("mean", t4)], writes=[("rstd6", t4)])
                P.op("dve", lambda e, s2=s2, rt=rt: e.scalar_tensor_tensor(out=rt, in0=s2, scalar=1.0 / D, in1=rt,
                                                                          op0=ALU.mult, op1=ALU.subtract),
                     reads=[("ps", s2k), ("rstd6", t4)], writes=[("rstd6", t4)])
                P.op("act", lambda e, rt=rt: e.activation(out=rt, in_=rt, func=AF.Sqrt, bias=epsb[:, :], scale=1.0),
                     reads=[("rstd6", t4), "epsb"], writes=[("rstd6", t4)])
                P.op("dve", lambda e, rt=rt: e.reciprocal(out=rt, in_=rt), reads=[("rstd6", t4)], writes=[("rstd6", t4)])

            P.barrier()
            W = Wz
            zitems = [(cc, t4) for cc in range(NCH) for t4 in range(4)]
            zacc = {}

            def z0(i):
                cc, t4 = zitems[i]
                a = tmp6[i % 6]
                P.op("dve", lambda e, a=a, cc=cc, t4=t4: e.tensor_tensor(
                    out=a, in0=ucT3[:, cc, t4 * 512:(t4 + 1) * 512], in1=meanA[:, t4 * 512:(t4 + 1) * 512], op=ALU.subtract),
                    reads=[("uc", cc, t4), ("mean", t4)], writes=[("tmp6", i % 6)])

            def z1(i):
                cc, t4 = zitems[i]
                a = tmp6[i % 6]
                P.op("pool", lambda e, a=a, t4=t4: e.tensor_tensor(out=a, in0=a, in1=rstdA[:, t4 * 512:(t4 + 1) * 512], op=ALU.mult),
                     reads=[("tmp6", i % 6), ("rstd6", t4)], writes=[("tmp6", i % 6)])

            def z2(i):
                cc, t4 = zitems[i]
                a = tmp6[i % 6]
                P.op("act", lambda e, a=a, cc=cc: e.activation(out=a, in_=a, func=AF.Silu,
                                                              bias=cvec[:, 2 * NCH + cc:2 * NCH + cc + 1],
                                                              scale=cvec[:, NCH + cc:NCH + cc + 1]),
                     reads=[("tmp6", i % 6), "cvec"], writes=[("tmp6", i % 6)])

            def z3(i):
                cc, t4 = zitems[i]
                wz, wzk = W.get(cc)
                acc, ak = PP[i % 4]

                def f(e, acc=acc, wz=wz, t4=t4):
                    ins = None
                    for kc in range(NCH):
                        ins = e.matmul(out=acc, lhsT=wz[:, kc, :], rhs=hT3[:, kc, t4 * 512:(t4 + 1) * 512],
                                       start=(kc == 0), stop=(kc == NCH - 1))
                    return ins
                P.op("pe", f, reads=[wzk] + hkeys(t4), writes=[("ps", ak)])
                zacc[i] = (acc, ak)

            def z4(i):
                acc, ak = zacc.pop(i)
                P.op("act", lambda e, acc=acc, i=i: e.activation(out=sg[i % 2][:, :], in_=acc, func=AF.Silu),
                     reads=[("ps", ak)], writes=[("sg", i % 2)])

            def z5(i):
                cc, t4 = zitems[i]
                a = tmp6[i % 6]
                P.op("dve", lambda e, a=a, i=i, cc=cc, t4=t4: e.tensor_tensor(
                    out=pT3[:, cc, t4 * 512:(t4 + 1) * 512], in0=a, in1=sg[i % 2][:, :], op=ALU.mult),
                    reads=[("tmp6", i % 6), ("sg", i % 2)], writes=[("pT", cc, t4)])

            run_pipeline([z0, z1, z2, z3, z4, z5], len(zitems))
            specs = []
            for c in range(NCH):
                specs.append(w_in[:, 8 * D + c * 128: 8 * D + (c + 1) * 128])
                specs.append(w_cv[:, c * 128:(c + 1) * 128])
            W6b = WStream(specs)
            W6b.ensure(NWS - 3)
            P.barrier()

            mT = uview(0, NCH * S, BF16)
            mT3 = mT.rearrange("p (c t) -> p c t", c=NCH)
            tA = uview(90112, 512, F32)
            tB = uview(90112 + 2048, 512, F32)
            W = W6b

            def pkeys(t4):
                return [("pT", c_, t4) for c_ in range(NCH)]
            k6 = 0
            for c in range(NCH):
                wg, wgk = W.get(2 * c)
                wy, wyk = W.get(2 * c + 1)
                for t4 in range(4):
                    accg, agk = proj_fm(wg, wgk, hT3, hkeys, t4)
                    accy, ayk = proj_fm(wy, wyk, pT3, pkeys, t4)
                    sgi = ev_i[0] % 2
                    ev_i[0] += 1
                    a = tA if k6 % 2 == 0 else tB
                    ak_ = "tA" if k6 % 2 == 0 else "tB"
                    k6 += 1
                    P.op("act", lambda e, accg=accg, sgi=sgi: e.activation(out=sg[sgi][:, :], in_=accg, func=AF.Sigmoid),
                         reads=[agk], writes=[("sg", sgi)])
                    P.op("dve", lambda e, a=a, accy=accy, sgi=sgi: e.tensor_tensor(out=a, in0=accy, in1=sg[sgi][:, :], op=ALU.mult),
                         reads=[ayk, ("sg", sgi)], writes=[ak_])
                    P.op("pool", lambda e, a=a, c=c, t4=t4: e.tensor_tensor(
                        out=a, in0=a, in1=m1T3[:, c, t4 * 512:(t4 + 1) * 512], op=ALU.add),
                        reads=[ak_, ("m1", c, t4)], writes=[ak_])
                    lo = S - (t4 + 1) * 512
                    P.op("dve", lambda e, a=a, c=c, lo=lo: e.tensor_copy(out=rev_free(mT3[:, c, lo:lo + 512]), in_=a),
                         reads=[ak_], writes=[("mT", c, 3 - t4)])
            wo = [uview(32768 + i * 8192, NCH * 512, BF16).rearrange("p (k n) -> p k n", k=NCH) for i in range(2)]
            NXT, NT1, NST = 6, 16, 5
            xts = [uview(49152 + i * 4096, 1024, F32) for i in range(NXT)]
            t1s = [R1[:, i * 2048:(i + 1) * 2048].bitcast(F32) for i in range(8)] + \
                  [R2[:, i * 2048:(i + 1) * 2048].bitcast(F32) for i in range(8)]
            g_in = uview(73728, 1024, F32)
            b_in = uview(73728 + 4096, 1024, F32)
            g_po = uview(73728 + 8192, 1024, F32)
            b_po = uview(73728 + 12288, 1024, F32)
            P.dma("sp", lambda e, g_in=g_in: e.dma_start(out=g_in, in_=ln_in_g.partition_broadcast(128)), writes=["gb"])
            P.dma("sp", lambda e, b_in=b_in: e.dma_start(out=b_in, in_=ln_in_b.partition_broadcast(128)), writes=["gb"])
            P.dma("sp", lambda e, g_po=g_po: e.dma_start(out=g_po, in_=ln_post_g.partition_broadcast(128)), writes=["gb"])
            P.dma("sp", lambda e, b_po=b_po: e.dma_start(out=b_po, in_=ln_post_b.partition_broadcast(128)), writes=["gb"])
            for i in range(2):
                P.dma("pool", lambda e, i=i, wo=wo: e.dma_start(
                    out=wo[i], in_=w_out[:, i * 512:(i + 1) * 512].rearrange("(k p) n -> p k n", p=128)),
                    writes=[("wo", i)])
            for tb in range(NXT):
                P.dma("sp", lambda e, xt=xts[tb], tb=tb, s=s: e.dma_start(out=xt, in_=x_d[s, tb * 128:(tb + 1) * 128, :]),
                      writes=[("xt", tb)])
            P.barrier()

            accs = {}

            junk = [uview(90112 + i * 2048, 1024, BF16) for i in range(2)]
            alphar = uview(94208, 128, F32)
            P.op("pool", lambda e, alphar=alphar: e.memset(alphar[0:1, :], ALPHA), writes=["alphar"])

            def r0(tb):
                if tb < NXT:
                    return
                xt = xts[tb % NXT]
                P.dma("sp", lambda e, xt=xt, tb=tb, s=s: e.dma_start(out=xt, in_=x_d[s, tb * 128:(tb + 1) * 128, :]),
                      writes=[("xt", tb % NXT)])

            def r1(tb):
                ln_norm(xts[tb % NXT], ("xt", tb % NXT), t1s[tb % NT1], ("t1", tb % NT1), 0,
                        rstdP[:, tb:tb + 1], "rstdP_ro", nmrP[:, tb:tb + 1], "nmrP_ro")

            def r2(tb):
                ln_mulg(t1s[tb % NT1], ("t1", tb % NT1), g_in)
                accs[tb] = []
                for half in range(2):
                    acc, ak = PP[(2 * tb + half) % 6]

                    def ffin(e, acc=acc, half=half, tb=tb, wo=wo, b_in=b_in, alphar=alphar):
                        for kc in range(NCH):
                            e.matmul(out=acc, lhsT=mT3[:, kc, tb * 128:(tb + 1) * 128], rhs=wo[half][:, kc, :],
                                     start=(kc == 0), stop=False)
                        return e.matmul(out=acc, lhsT=alphar[0:1, :], rhs=b_in[0:1, half * 512:(half + 1) * 512],
                                        start=False, stop=True)
                    P.op("pe", ffin, reads=[("mT", c_, tb // 4) for c_ in range(NCH)] + [("wo", half), "gb", "alphar"], writes=[("ps", ak)])
                    accs[tb].append((acc, ak))

            def r3(tb):
                pass

            def r4(tb):
                t1 = t1s[tb % NT1]
                t1k = ("t1", tb % NT1)
                for half, (acc, ak) in enumerate(accs.pop(tb)):
                    P.op("dve", lambda e, acc=acc, half=half, t1=t1: e.scalar_tensor_tensor(
                        out=t1[:, half * 512:(half + 1) * 512], in0=t1[:, half * 512:(half + 1) * 512], scalar=ALPHA,
                        in1=acc, op0=ALU.mult, op1=ALU.add),
                        reads=[("ps", ak), t1k], writes=[t1k])

            def r5(tb):
                b = NST + tb % NST
                t1 = t1s[tb % NT1]
                P.op("act", lambda e, t1=t1, b=b: e.activation(out=junk[0], in_=t1, func=AF.Identity, accum_out=mv[b][:, 0:1]),
                     reads=[("t1", tb % NT1)], writes=[("s1", b), "junk0"])
                P.op("act", lambda e, t1=t1, b=b: e.activation(out=junk[1], in_=t1, func=AF.Square, accum_out=mv[b][:, 1:2]),
                     reads=[("t1", tb % NT1)], writes=[("s2", b), "junk1"])

            def r6(tb):
                b = NST + tb % NST
                P.op("dve", lambda e, b=b: e.tensor_single_scalar(out=mv[b][:, 0:1], in_=mv[b][:, 0:1], scalar=1.0 / D, op=ALU.mult),
                     reads=[("s1", b)], writes=[("s1", b)])
                P.op("dve", lambda e, b=b: e.tensor_tensor(out=st6[b][:, 0:1], in0=mv[b][:, 0:1], in1=mv[b][:, 0:1], op=ALU.mult),
                     reads=[("s1", b)], writes=[("msq", b)])
                P.op("dve", lambda e, b=b: e.scalar_tensor_tensor(out=mv[b][:, 1:2], in0=mv[b][:, 1:2], scalar=1.0 / D,
                                                                 in1=st6[b][:, 0:1], op0=ALU.mult, op1=ALU.subtract),
                     reads=[("s2", b), ("msq", b)], writes=[("mv", b)])

            def r7(tb):
                ln_sqrt(NST + tb % NST)

            def r8(tb):
                b = NST + tb % NST
                P.op("dve", lambda e, b=b: e.reciprocal(out=rstd[b][:, :], in_=rstd[b][:, :]),
                     reads=[("rstd", b)], writes=[("rstd", b)])
                P.op("dve", lambda e, b=b: e.scalar_tensor_tensor(out=nmr[b][:, :], in0=mv[b][:, 0:1], scalar=-1.0,
                                                                 in1=rstd[b][:, :], op0=ALU.mult, op1=ALU.mult),
                     reads=[("s1", b), ("rstd", b)], writes=[("nmr", b)])

            def r9(tb):
                ln_norm(t1s[tb % NT1], ("t1", tb % NT1), t1s[tb % NT1], ("t1", tb % NT1), NST + tb % NST)

            def r10(tb):
                ln_mulg(t1s[tb % NT1], ("t1", tb % NT1), g_po)

            def r11(tb):
                t1 = t1s[tb % NT1]
                P.op("dve", lambda e, t1=t1: e.tensor_tensor(out=t1, in0=t1, in1=b_po, op=ALU.add),
                     reads=[("t1", tb % NT1), "gb"], writes=[("t1", tb % NT1)])

            def r12(tb):
                t1 = t1s[tb % NT1]
                P.dma("sp", lambda e, t1=t1, tb=tb, s=s: e.dma_start(out=y_d[s, tb * 128:(tb + 1) * 128, :], in_=t1),
                      reads=[("t1", tb % NT1)], writes=[("y", tb)])

            def r1011(tb):
                r10(tb)
                r11(tb)
            run_pipeline([r0, r1, r2, r4, r5, r6, r7, r8, r9, r1011, r12], NB)
            P.barrier()

        block = E(nc.Block())
        P.replay(block)
    return nc


_CACHE = {}


def kernel(**inputs):
    n = 8
    x = np.ascontiguousarray(inputs["x"], dtype=np.float32)
    shards = x.reshape(n, NSEQ, S, D)
    common = {
        "ln_in_g": np.ascontiguousarray(inputs["ln_in_g"], dtype=np.float32),
        "ln_in_b": np.ascontiguousarray(inputs["ln_in_b"], dtype=np.float32),
        "w_in": np.ascontiguousarray(inputs["w_in"][0], dtype=np.float32),
        "w_sb_proj": np.ascontiguousarray(inputs["w_sb_proj"][0], dtype=np.float32),
        "conv_w": np.ascontiguousarray(inputs["conv_w"][0], dtype=np.float32),
        "conv_b": np.ascontiguousarray(inputs["conv_b"][0], dtype=np.float32),
        "conv_ln_g": np.ascontiguousarray(inputs["conv_ln_g"][0], dtype=np.float32),
        "conv_ln_b": np.ascontiguousarray(inputs["conv_ln_b"][0], dtype=np.float32),
        "w_cv_proj": np.ascontiguousarray(inputs["w_cv_proj"][0], dtype=np.float32),
        "w_out": np.ascontiguousarray(inputs["w_out"][0], dtype=np.float32),
        "ln_post_g": np.ascontiguousarray(inputs["ln_post_g"][0], dtype=np.float32),
        "ln_post_b": np.ascontiguousarray(inputs["ln_post_b"][0], dtype=np.float32),
    }
    nc = build_program()
    in_maps = []
    for i in range(n):
        m = dict(common)
        m["x"] = np.ascontiguousarray(shards[i])
        in_maps.append(m)
    res = run_bass_kernel_spmd(nc, in_maps, core_ids=list(range(n)))
    out = np.concatenate([np.asarray(r["y"]) for r in res.results], axis=0)
    return out.reshape(16, S, D).astype(np.float32)
```
